# Optimizing a Trainium2 kernel written in Bass

```python
import math
import jax, jax.numpy as jnp
from jax import lax
import numpy as np

D_MODEL = 1024
BATCH = 16
SEQ = 256
DEPTH = 1
DEC_BATCH = 4
DEC_SEQ = 4096
PAST_LEN = 512

GRID_W = 64
D_ATT = D_MODEL
HD_QK = 64
HD_V = 2 * HD_QK
H_ATT = D_ATT // HD_V
D_SSD = D_MODEL
SSD_HEADDIM = 64
H_SSD = D_SSD // SSD_HEADDIM
SSD_GROUPS = 4
HEADS_PER_GROUP = H_SSD // SSD_GROUPS
D_STATE = 64
CONV_K = 3
CONV_DIM = D_SSD + 2 * SSD_GROUPS * D_STATE
D_INNER = D_ATT + D_SSD
D_PROJ = 4 * D_ATT + D_SSD + CONV_DIM + 2 * H_SSD
CHUNK = 128
Q_BLOCK = 128
ROPE_BASE = 10000.0
ROPE_FREQS = HD_QK // 4
EPS = 1e-6

kernel_name = "hymba_diffattn_ssd_prefix_diffusion_step"

F32 = jnp.float32


def rms_norm(x, g):
    xf = x.astype(F32)
    y = xf * lax.rsqrt(jnp.mean(xf * xf, axis=-1, keepdims=True) + EPS)
    return (y * g.astype(F32)).astype(x.dtype)


def adaln(cond, w_mod, b_mod):
    m = jax.nn.silu(cond) @ w_mod + b_mod
    shift, scale, gate = jnp.split(m, 3, axis=-1)
    return shift[:, None, :], scale[:, None, :], gate[:, None, :]


def split_proj(u):
    idx = np.cumsum([D_ATT, D_ATT, D_ATT, D_ATT, D_SSD, CONV_DIM]).tolist()
    return jnp.split(u, idx, axis=-1)


def axial_rope_tables(L):
    rows = L // GRID_W
    row_ids = jnp.repeat(jnp.arange(rows), GRID_W).astype(F32)
    col_ids = jnp.tile(jnp.arange(GRID_W), rows).astype(F32)
    inv = ROPE_BASE ** (-jnp.arange(ROPE_FREQS, dtype=F32) / ROPE_FREQS)
    ang_r = row_ids[:, None] * inv
    ang_c = col_ids[:, None] * inv
    return jnp.cos(ang_r), jnp.sin(ang_r), jnp.cos(ang_c), jnp.sin(ang_c)


def rotate(x, cos, sin):
    x1, x2 = jnp.split(x, 2, axis=-1)
    return jnp.concatenate([x1 * cos - x2 * sin, x2 * cos + x1 * sin], axis=-1)


def apply_rope(x, tables):
    cr, sr, cc, sc = [t.astype(x.dtype)[:, None, None, :] for t in tables]
    xr, xc = jnp.split(x, 2, axis=-1)
    return jnp.concatenate([rotate(xr, cr, sr), rotate(xc, cc, sc)], axis=-1)


def diff_attention(q, k, v, lam):
    b, Lq = q.shape[0], q.shape[1]
    nb = Lq // Q_BLOCK
    qb = q.reshape(b, nb, Q_BLOCK, H_ATT, 2, HD_QK).transpose(1, 0, 2, 3, 4, 5)
    scale = 1.0 / math.sqrt(HD_QK)

    def block(qi):
        s = jnp.einsum('bqhcd,bkhcd->bchqk', qi, k).astype(F32) * scale
        p = jax.nn.softmax(s, axis=-1)
        a = p[:, 0] - lam * p[:, 1]
        return jnp.einsum('bhqk,bkhd->bqhd', a.astype(v.dtype), v)

    o = lax.map(block, qb)
    return o.transpose(1, 0, 2, 3, 4).reshape(b, Lq, H_ATT, HD_V)


def centred_conv(u, w, bias):
    L = u.shape[1]
    pad = CONV_K // 2
    up = jnp.pad(u, ((0, 0), (pad, pad), (0, 0)))
    out = bias
    for j in range(CONV_K):
        out = out + up[:, j:j + L] * w[j]
    return out


def ssd_scan(x, dt, A, B, C, h0):
    b, L, H, P = x.shape
    N = B.shape[-1]
    nc = L // CHUNK
    xc = x.astype(F32).reshape(b, nc, CHUNK, H, P)
    dtc = dt.reshape(b, nc, CHUNK, H)
    Bc = B.astype(F32).reshape(b, nc, CHUNK, H, N)
    Cc = C.astype(F32).reshape(b, nc, CHUNK, H, N)
    a_cum = jnp.cumsum(dtc * A, axis=2)
    seg = a_cum[:, :, :, None, :] - a_cum[:, :, None, :, :]
    mask = jnp.tril(jnp.ones((CHUNK, CHUNK), dtype=bool))[None, None, :, :, None]
    Lmat = jnp.exp(jnp.where(mask, seg, -jnp.inf))
    xdt = xc * dtc[..., None]
    cb = jnp.einsum('bcqhn,bcshn->bcqsh', Cc, Bc)
    y_diag = jnp.einsum('bcqsh,bcshp->bcqhp', cb * Lmat, xdt)
    decay_to_end = jnp.exp(a_cum[:, :, -1:, :] - a_cum)
    states = jnp.einsum('bcqhn,bcqh,bcqhp->bchpn', Bc, decay_to_end, xdt)
    chunk_decay = jnp.exp(a_cum[:, :, -1, :])

    def step(h, inp):
        st, dec = inp
        return h * dec[:, :, None, None] + st, h

    h_final, h_prev = lax.scan(step, h0.astype(F32),
                               (states.transpose(1, 0, 2, 3, 4), chunk_decay.transpose(1, 0, 2)))
    h_prev = h_prev.transpose(1, 0, 2, 3, 4)
    y_off = jnp.einsum('bcqhn,bchpn,bcqh->bcqhp', Cc, h_prev, jnp.exp(a_cum))
    return (y_diag + y_off).reshape(b, L, H, P), h_final


def ssd_branch(xbc_raw, dt_raw, z, conv_w, conv_b, dt_bias, A_log, D_skip, norm_g, h0):
    b, L, _ = xbc_raw.shape
    xbc = jax.nn.silu(centred_conv(xbc_raw, conv_w, conv_b))
    xs, Bm, Cm = jnp.split(xbc, [D_SSD, D_SSD + SSD_GROUPS * D_STATE], axis=-1)
    xs = xs.reshape(b, L, H_SSD, SSD_HEADDIM)
    Bm = jnp.repeat(Bm.reshape(b, L, SSD_GROUPS, D_STATE), HEADS_PER_GROUP, axis=2)
    Cm = jnp.repeat(Cm.reshape(b, L, SSD_GROUPS, D_STATE), HEADS_PER_GROUP, axis=2)
    dt = jax.nn.softplus(dt_raw.reshape(b, L, 2, H_SSD).astype(F32) + dt_bias.astype(F32))
    A = -jnp.exp(A_log.astype(F32))
    y_f, h_f = ssd_scan(xs, dt[:, :, 0], A[0], Bm, Cm, h0[:, 0])
    y_b, h_b = ssd_scan(jnp.flip(xs, 1), jnp.flip(dt[:, :, 1], 1), A[1],
                        jnp.flip(Bm, 1), jnp.flip(Cm, 1), h0[:, 1])
    y = y_f + jnp.flip(y_b, 1) + D_skip.astype(F32)[:, None] * xs.astype(F32)
    y = y.reshape(b, L, D_SSD).astype(z.dtype) * jax.nn.silu(z)
    return rms_norm(y, norm_g), jnp.stack([h_f, h_b], axis=1)


def layer(x, cond, w_mod, b_mod, norm_g, w_in, lq1, lk1, lq2, lk2, subln_g,
          conv_w, conv_b, dt_bias, A_log, D_skip, ssd_norm_g, w_out, lam_init,
          rope, k_prefix, v_prefix, h0):
    b, L, _ = x.shape
    shift, scale, gate = adaln(cond, w_mod, b_mod)
    h = rms_norm(x, norm_g) * (1.0 + scale) + shift
    q, k, v, g_att, z, xbc, dt_raw = split_proj(h @ w_in)
    q = q.reshape(b, L, H_ATT, 2, HD_QK)
    k = k.reshape(b, L, H_ATT, 2, HD_QK)
    v = v.reshape(b, L, H_ATT, HD_V)
    if rope is not None:
        q = apply_rope(q, rope)
        k = apply_rope(k, rope)
    k_all, v_all = k, v
    if k_prefix is not None:
        k_all = jnp.concatenate([k, k_prefix], axis=1)
        v_all = jnp.concatenate([v, v_prefix], axis=1)
    lam = (jnp.exp(jnp.sum(lq1.astype(F32) * lk1.astype(F32)))
           - jnp.exp(jnp.sum(lq2.astype(F32) * lk2.astype(F32))) + lam_init)
    att = diff_attention(q, k_all, v_all, lam)
    att = rms_norm(att, subln_g) * (1.0 - lam_init)
    att = att.reshape(b, L, D_ATT) * jax.nn.silu(g_att)
    if h0 is None:
        h0 = jnp.zeros((b, 2, H_SSD, SSD_HEADDIM, D_STATE), F32)
    ssd_y, h_fin = ssd_branch(xbc, dt_raw, z, conv_w, conv_b, dt_bias, A_log, D_skip,
                              ssd_norm_g, h0.astype(F32))
    out = jnp.concatenate([att, ssd_y], axis=-1) @ w_out
    return x + gate * out, k.reshape(b, L, H_ATT, 2 * HD_QK), v, h_fin


def setup_inputs(seed: int = 0) -> dict:
    key = jax.random.key(seed)
    ks = jax.random.split(key, 26)
    n = jax.random.normal
    dt0 = jnp.exp(jax.random.uniform(ks[18], (DEPTH, 2, H_SSD), minval=math.log(1e-3), maxval=math.log(1e-1)))
    return {
        "x_prompt": n(ks[0], (BATCH, SEQ, D_MODEL), F32),
        "x_sample": n(ks[1], (DEC_BATCH, DEC_SEQ, D_MODEL), F32),
        "cache_k": n(ks[2], (DEC_BATCH, DEPTH, PAST_LEN, H_ATT, 2 * HD_QK), F32),
        "cache_v": n(ks[3], (DEC_BATCH, DEPTH, PAST_LEN, H_ATT, HD_V), F32),
        "state_ssd": 0.5 * n(ks[4], (DEC_BATCH, DEPTH, 2, H_SSD, SSD_HEADDIM, D_STATE), F32),
        "c": n(ks[5], (DEC_BATCH, D_MODEL), F32),
        "c_ctx": n(ks[6], (D_MODEL,), F32),
        "w_mod": n(ks[7], (DEPTH, D_MODEL, 3 * D_MODEL), F32) * D_MODEL ** -0.5,
        "b_mod": 0.01 * n(ks[8], (DEPTH, 3 * D_MODEL), F32),
        "norm_g": 1.0 + 0.02 * n(ks[9], (DEPTH, D_MODEL), F32),
        "w_in": n(ks[10], (DEPTH, D_MODEL, D_PROJ), F32) * D_MODEL ** -0.5,
        "lambda_q1": 0.1 * n(ks[11], (DEPTH, HD_QK), F32),
        "lambda_k1": 0.1 * n(ks[12], (DEPTH, HD_QK), F32),
        "lambda_q2": 0.1 * n(ks[13], (DEPTH, HD_QK), F32),
        "lambda_k2": 0.1 * n(ks[14], (DEPTH, HD_QK), F32),
        "subln_g": 1.0 + 0.02 * n(ks[15], (DEPTH, HD_V), F32),
        "conv_w": n(ks[16], (DEPTH, CONV_K, CONV_DIM), F32) * CONV_K ** -0.5,
        "conv_b": 0.01 * n(ks[17], (DEPTH, CONV_DIM), F32),
        "dt_bias": dt0 + jnp.log(-jnp.expm1(-dt0)),
        "A_log": jnp.log(jax.random.uniform(ks[19], (DEPTH, 2, H_SSD), minval=1.0, maxval=16.0)),
        "D_skip": 1.0 + 0.02 * n(ks[20], (DEPTH, H_SSD), F32),
        "ssd_norm_g": 1.0 + 0.02 * n(ks[21], (DEPTH, D_SSD), F32),
        "w_out": n(ks[22], (DEPTH, D_INNER, D_MODEL), F32) * D_INNER ** -0.5,
        "final_g": 1.0 + 0.02 * n(ks[23], (D_MODEL,), F32),
    }


def reference(x_prompt, x_sample, cache_k, cache_v, state_ssd, c, c_ctx, w_mod, b_mod,
              norm_g, w_in, lambda_q1, lambda_k1, lambda_q2, lambda_k2, subln_g, conv_w,
              conv_b, dt_bias, A_log, D_skip, ssd_norm_g, w_out, final_g):
    b_dec, l_dec = x_sample.shape[0], x_sample.shape[1]
    l_past = cache_k.shape[2]
    rope = axial_rope_tables(l_dec)
    xp, xs = x_prompt, x_sample
    ks_new, vs_new, hs_new = [], [], []
    for l in range(DEPTH):
        lam_init = 0.8 - 0.6 * math.exp(-0.3 * l)
        p = (w_mod[l], b_mod[l], norm_g[l], w_in[l], lambda_q1[l], lambda_k1[l],
             lambda_q2[l], lambda_k2[l], subln_g[l], conv_w[l], conv_b[l], dt_bias[l],
             A_log[l], D_skip[l], ssd_norm_g[l], w_out[l], lam_init)
        xp, k_ctx, v_ctx, h_ctx = layer(xp, c_ctx[None, :], *p, None, None, None, None)
        ks_new.append(k_ctx)
        vs_new.append(v_ctx)
        hs_new.append(h_ctx)
        kp = cache_k[:, l].reshape(b_dec, l_past, H_ATT, 2, HD_QK)
        vp = cache_v[:, l]
        xs, _, _, _ = layer(xs, c, *p, rope, kp, vp, state_ssd[:, l])
    y_prompt = rms_norm(xp, final_g)
    y_sample = rms_norm(xs, final_g)
    new_cache_k = jnp.stack(ks_new, axis=1)
    new_cache_v = jnp.stack(vs_new, axis=1)
    new_state_ssd = jnp.stack(hs_new, axis=1)
    return (y_prompt, y_sample, new_cache_k, new_cache_v, new_state_ssd)
```

```python
import math
import numpy as np
import concourse.bass as bass
import concourse.mybir as mybir
from concourse.bass_utils import run_bass_kernel_spmd

F32 = mybir.dt.float32
BF16 = mybir.dt.bfloat16
AF = mybir.ActivationFunctionType
ALU = mybir.AluOpType

EPS = 1e-6
NCORES = 8
NTILE = 36
NTOK = NTILE * 128
NOWN = 2560
NEGBIG = -30000.0
NCOL = 576 + 2048 + 6144
C_NG, C_CWS, C_CWC, C_CB, C_DTBS, C_DTBC, C_ALS, C_ALC, C_D, C_SG, C_SUB, C_LAM = (
    0, 8, 44, 80, 92, 124, 156, 188, 220, 228, 236, 237)
NCST = 237 + 256


class T:
    __slots__ = ("name", "w", "r", "ld", "st")

    def __init__(self, name):
        self.name = name
        self.w = None
        self.r = {}
        self.ld = None
        self.st = None


class Sched:
    ENG = ("pe", "act", "dve", "pool", "sp")

    def __init__(self, nc):
        self.nc = nc
        self.prog = {e: [] for e in self.ENG}
        self.cnt = {e: 0 for e in ("pe", "act", "dve", "pool")}
        self.waited = {e: {} for e in self.ENG}
        self.sems = {}
        self.nsem = 0
        self._stack = []
        self.dma_keys = {}
        for e in ("pe", "act", "dve", "pool"):
            self._newsem(e)
        self.final = []

    def _newsem(self, key):
        cm = self.nc.semaphore("s%d" % self.nsem)
        h = cm.__enter__()
        self._stack.append(cm)
        self.sems[key] = h
        self.nsem += 1
        return h

    def _deps(self, eng, reads, writes):
        deps = {}

        def add(k, c, kind):
            if k == eng:
                if eng == "pe" or kind == "waw":
                    return
            if deps.get(k, 0) < c:
                deps[k] = c
        for t in reads:
            if t.w is not None:
                add(t.w[0], t.w[1], "raw")
        for t in writes:
            if t.w is not None:
                add(t.w[0], t.w[1], "waw")
            for k, c in t.r.items():
                add(k, c, "war")
        return deps

    def _emit_waits(self, eng, deps):
        wd = self.waited[eng]
        for k, c in deps.items():
            if wd.get(k, 0) >= c:
                continue
            wd[k] = c
            h = self.sems[k]
            self.prog[eng].append(lambda e, h=h, c=c: e.wait_ge(h, c))

    def op(self, eng, fn, reads=(), writes=()):
        self._emit_waits(eng, self._deps(eng, reads, writes))
        self.cnt[eng] += 1
        c = self.cnt[eng]
        h = self.sems[eng]
        self.prog[eng].append(lambda e, fn=fn, h=h: fn(e).then_inc(h, 1))
        for t in writes:
            t.w = (eng, c)
            t.r = {}
        for t in reads:
            t.r[eng] = c

    def dma_load(self, q, tile, parts, reads=()):
        self._emit_waits(q, self._deps(q, reads, [tile]))
        if tile.ld is None:
            key = ("ld", self.nsem)
            self._newsem(key)
            tile.ld = [key, 0]
        key = tile.ld[0]
        h = self.sems[key]
        for (o, i) in parts:
            tile.ld[1] += 16
            self.prog[q].append(lambda e, o=o, i=i, h=h: e.dma_start(out=o, in_=i).then_inc(h, 16))
        tile.w = (key, tile.ld[1])
        tile.r = {}
        for t in reads:
            t.r[key] = tile.ld[1]
        self.dma_keys[key] = tile.ld[1]

    def dma_store(self, q, tile, parts, final=True, dram_t=None):
        wr = [dram_t] if dram_t is not None else []
        self._emit_waits(q, self._deps(q, [tile], wr))
        if tile.st is None:
            key = ("st", self.nsem)
            self._newsem(key)
            tile.st = [key, 0]
        key = tile.st[0]
        h = self.sems[key]
        for (o, i) in parts:
            tile.st[1] += 16
            self.prog[q].append(lambda e, o=o, i=i, h=h: e.dma_start(out=o, in_=i).then_inc(h, 16))
        tile.r[key] = tile.st[1]
        if dram_t is not None:
            dram_t.w = (key, tile.st[1])
            dram_t.r = {}
        self.dma_keys[key] = tile.st[1]

    def barrier(self):
        tgt = {e: c for e, c in self.cnt.items() if c > 0}
        tgt.update(self.dma_keys)
        for e in self.ENG:
            self._emit_waits(e, tgt)

    def finish(self):
        nc = self.nc
        self.barrier()
        prog = self.prog
        with nc.allow_non_contiguous_dma(reason="small strided constant/state transfers"), nc.Block() as block:
            @block.sync
            def _(e):
                for f in prog["sp"]:
                    f(e)

            @block.tensor
            def _(e):
                for f in prog["pe"]:
                    f(e)

            @block.scalar
            def _(e):
                for f in prog["act"]:
                    f(e)

            @block.vector
            def _(e):
                for f in prog["dve"]:
                    f(e)

            @block.gpsimd
            def _(e):
                for f in prog["pool"]:
                    f(e)
        for cm in reversed(self._stack):
            cm.__exit__(None, None, None)
        self._stack = []


def mm(out, lhsT, rhs, start=True, stop=True):
    return lambda e: e.matmul(out, lhsT=lhsT, rhs=rhs, start=start, stop=stop)


def tr(out, in_, ident):
    return lambda e: e.transpose(out=out, in_=in_, identity=ident)


def act(out, in_, func, scale=None, bias=None, accum_out=None):
    kw = {}
    if scale is not None:
        kw["scale"] = scale
    if bias is not None:
        kw["bias"] = bias
    if accum_out is not None:
        kw["accum_out"] = accum_out
    return lambda e: e.activation(out=out, in_=in_, func=func, **kw)


def tt(out, in0, in1, op):
    return lambda e: e.tensor_tensor(out=out, in0=in0, in1=in1, op=op)


def ts(out, in0, s1, s2, op0, op1=None):
    if op1 is None:
        return lambda e: e.tensor_scalar(out=out, in0=in0, scalar1=s1, scalar2=None, op0=op0)
    return lambda e: e.tensor_scalar(out=out, in0=in0, scalar1=s1, scalar2=s2, op0=op0, op1=op1)


def stt(out, in0, scalar, in1, op0, op1):
    return lambda e: e.scalar_tensor_tensor(out=out, in0=in0, scalar=scalar, in1=in1, op0=op0, op1=op1)


def cp(out, in_):
    return lambda e: (e.tensor_copy(out=out, in_=in_) if hasattr(e, "tensor_copy") else e.copy(out=out, in_=in_))


def ms(ap, v):
    return lambda e: e.memset(ap, v)


class Arena:
    def __init__(self, nc, words):
        self.cm = nc.sbuf_tensor("arena", [128, words], F32)
        self.t = self.cm.__enter__()
        self.words = words
        self.top = 0
        self.peak = 0

    def alloc(self, shape, dt):
        n = 1
        for s in shape[1:]:
            n *= s
        if dt == F32:
            w = n
        else:
            w = (n + 1) // 2
        assert self.top + w <= self.words, ("SBUF arena overflow", self.top, w, self.words)
        v = self.t[:, self.top:self.top + w]
        if dt != F32:
            v = v.bitcast(dt)[:, 0:n]
        self.top += w
        self.peak = max(self.peak, self.top)
        if len(shape) == 3:
            v = v.rearrange("p (a b) -> p a b", a=shape[1])
        elif len(shape) == 4:
            v = v.rearrange("p (a b c) -> p a b c", a=shape[1], b=shape[2])
        if shape[0] != 128:
            v = v[0:shape[0]]
        return v

    def mark(self):
        return self.top

    def release(self, m):
        self.top = m

    def close(self):
        self.cm.__exit__(None, None, None)


def pcol(t):
    if t < 4096:
        return t + 2
    if t < 4352:
        return t + 4
    return t + 6


def oidx(t):
    return t if t < 2048 else t - 2048


OWN_TILES = list(range(16)) + [32, 33, 34, 35]
OWN_STS = [0, 1, 2, 3, 8]
SEQS = [(0, 32, 16, True, -1), (32, 2, 2, False, 0), (34, 2, 2, False, 1)]


def otile(t):
    return t if t < 16 else t - 16


class _Stop(Exception):
    pass


def build_program(stop=None):
    nc = bass.Bass("TRN2", target_bir_lowering=False)
    env = {}
    try:
        _build_body(nc, stop, env)
    except _Stop:
        pass
    S, pcm, A = env["S"], env["pcm"], env["A"]
    _STATS.update(cnt=dict(S.cnt), nsem=S.nsem, peak_words=A.peak, nprog={k: len(v) for k, v in S.prog.items()})
    S.finish()
    pcm.__exit__(None, None, None)
    A.close()
    return nc


def _build_body(nc, stop, env):
    def chk(x):
        if stop is not None and abs(stop - x) < 1e-9:
            raise _Stop()

    def din(name, shape):
        return nc.dram_tensor(name, shape, F32, kind="ExternalInput").ap()

    def dout(name, shape):
        return nc.dram_tensor(name, shape, F32, kind="ExternalOutput").ap()

    d_xs = din("xs", [4096, 1024])
    d_xc = din("xc", [512, 1024])
    d_cond = din("cond2", [128, 8, 2])
    d_wmod = din("wmod", [1024, 3072])
    d_bmod = din("bmod2", [2, 3072])
    d_wall = din("wall", [1024, NCOL])
    d_wout = din("wout", [2048, 1024])
    d_cos = din("cosT", [128, 4096])
    d_sin = din("sinT", [128, 4096])
    d_ck = din("ck", [512, 8, 128])
    d_cv = din("cv", [512, 8, 128])
    d_h0 = din("h0", [2, 16, 64, 64])
    d_cst = din("cst", [128, NCST])
    d_fg = din("fgbc", [128, 1024])
    d_ys = dout("ys", [2048, 1024])
    d_yc = dout("yc", [512, 1024])
    d_ko = dout("ko", [512, 8, 128])
    d_vo = dout("vo", [512, 8, 128])
    d_ho = dout("ho", [2, 2, 16, 64, 64])
    d_mscr = nc.dram_tensor("mscr", [2, 3072], F32, kind="Internal").ap()
    d_yt = nc.dram_tensor("ytscr", [16, 128, NOWN], BF16, kind="Internal").ap()
    Tmscr = T("mscr")
    Tyt = [T("yt%d" % i) for i in range(16)]

    wallv = d_wall.rearrange("(kc p) n -> p kc n", p=128)

    S = Sched(nc)
    A = Arena(nc, 53200)
    pcm = nc.psum_tensor("PS", [128, 4096], F32)
    PS = pcm.__enter__()
    env.update(S=S, pcm=pcm, A=A)

    def bank(i, n=512, off=0):
        return PS[:, i * 512 + off:i * 512 + off + n]

    hT = A.alloc([128, 8, NTOK], BF16)
    ThT = [T("hT%d" % t) for t in range(NTILE)]
    cst = A.alloc([128, NCST], F32)
    Tcst = T("cst")
    ident_f = A.alloc([128, 128], F32)
    ident_b = A.alloc([128, 128], BF16)
    ones_f = A.alloc([128, 128], F32)
    ones_b = A.alloc([128, 128], BF16)
    mask_f = A.alloc([128, 128], F32)
    mask_b = A.alloc([128, 128], F32)
    neg_f = A.alloc([128, 128], F32)
    neg_b = A.alloc([128, 128], F32)
    Tconst = T("consts")
    A1 = A.alloc([128, 2, 8], F32)
    SH = A.alloc([128, 2, 8], F32)
    Tmod = T("mod")
    neglam = A.alloc([128, 1], F32)
    sublnc = A.alloc([128, 1], F32)
    negA_s = A.alloc([128, 32], F32)
    negA_c = A.alloc([128, 32], F32)
    Tsm = T("smallconsts")
    ssq_tok = A.alloc([128, 20], F32)
    Tssqt = T("ssq_tok")

    ngcol = cst[:, C_NG:C_NG + 8]
    cw_s = cst[:, C_CWS:C_CWS + 36].rearrange("p (a b) -> p a b", a=12)
    cw_c = cst[:, C_CWC:C_CWC + 36].rearrange("p (a b) -> p a b", a=12)
    cbias = cst[:, C_CB:C_CB + 12]
    dtb_s = cst[:, C_DTBS:C_DTBS + 32]
    dtb_c = cst[:, C_DTBC:C_DTBC + 32]
    alog_s = cst[:, C_ALS:C_ALS + 32]
    alog_c = cst[:, C_ALC:C_ALC + 32]
    dcol = cst[:, C_D:C_D + 8]
    sgcol = cst[:, C_SG:C_SG + 8]
    subln = cst[:, C_SUB:C_SUB + 1]
    lamv = cst[:, C_LAM:C_LAM + 256].rearrange("p (a b) -> p a b", a=4)

    Tb = [T("bank%d" % i) for i in range(8)]
    Tpsb = [T("psbA"), T("psbB")]

    S.dma_load("sp", Tcst, [(cst, d_cst)])
    S.op("pool", ms(ident_f, 1.0), writes=[Tconst])
    S.op("pool", lambda e: e.affine_select(out=ident_f, in_=ident_f, pattern=[[1, 128]], compare_op=ALU.is_equal,
                                           fill=0.0, base=0, channel_multiplier=-1), reads=[Tconst], writes=[Tconst])
    S.op("pool", cp(ident_b, ident_f), reads=[Tconst], writes=[Tconst])
    S.op("pool", ms(ones_f, 1.0), writes=[Tconst])
    S.op("pool", ms(ones_b, 1.0), writes=[Tconst])
    S.op("pool", ms(mask_f, 1.0), writes=[Tconst])
    S.op("pool", lambda e: e.affine_select(out=mask_f, in_=mask_f, pattern=[[1, 128]], compare_op=ALU.is_ge,
                                           fill=0.0, base=0, channel_multiplier=-1), reads=[Tconst], writes=[Tconst])
    S.op("pool", ms(mask_b, 1.0), writes=[Tconst])
    S.op("pool", lambda e: e.affine_select(out=mask_b, in_=mask_b, pattern=[[-1, 128]], compare_op=ALU.is_ge,
                                           fill=0.0, base=0, channel_multiplier=1), reads=[Tconst], writes=[Tconst])
    S.op("pool", ms(neg_f, 0.0), writes=[Tconst])
    S.op("pool", lambda e: e.affine_select(out=neg_f, in_=neg_f, pattern=[[1, 128]], compare_op=ALU.is_ge,
                                           fill=NEGBIG, base=0, channel_multiplier=-1), reads=[Tconst], writes=[Tconst])
    S.op("pool", ms(neg_b, 0.0), writes=[Tconst])
    S.op("pool", lambda e: e.affine_select(out=neg_b, in_=neg_b, pattern=[[-1, 128]], compare_op=ALU.is_ge,
                                           fill=NEGBIG, base=0, channel_multiplier=1), reads=[Tconst], writes=[Tconst])
    S.op("pool", ms(ssq_tok, 0.0), writes=[Tssqt])

    m0 = A.mark()
    m1 = A.mark()
    xt = [A.alloc([128, 1024], F32) for _ in range(4)]
    Txt = [T("xt%d" % i) for i in range(4)]
    xh = [A.alloc([128, 1024], BF16) for _ in range(4)]
    Txh = [T("xh%d" % i) for i in range(4)]
    junk = A.alloc([128, 1024], BF16)
    Tjunk = T("junk")
    st1 = A.alloc([128, 3, NTILE], F32)
    Tst1 = [T("st1_%d" % t) for t in range(NTILE)]
    def ph1_A(t):
        sl = t % 4
        src = d_xs[t * 128:(t + 1) * 128, :] if t < 32 else d_xc[(t - 32) * 128:(t - 31) * 128, :]
        S.dma_load("sp", Txt[sl], [(xt[sl], src)])
        S.op("act", act(junk, xt[sl], AF.Square, accum_out=st1[:, 0, t:t + 1]), reads=[Txt[sl]], writes=[Tjunk, Tst1[t]])
        S.op("act", act(st1[:, 1, t:t + 1], st1[:, 0, t:t + 1], AF.Ln, scale=1.0 / 1024, bias=EPS), reads=[Tst1[t]], writes=[Tst1[t]])
        S.op("act", act(st1[:, 2, t:t + 1], st1[:, 1, t:t + 1], AF.Exp, scale=-0.5), reads=[Tst1[t]], writes=[Tst1[t]])
        S.op("dve", ts(xh[sl], xt[sl], st1[:, 2, t:t + 1], None, ALU.mult), reads=[Txt[sl], Tst1[t]], writes=[Txh[sl]])

    def ph1_B(t):
        sl = t % 4
        c = 0 if t < 32 else 1
        pbk = t % 4
        psb = bank(pbk).bitcast(BF16)
        for kc in range(8):
            S.op("pe", tr(psb[:, kc * 128:(kc + 1) * 128], xh[sl][:, kc * 128:(kc + 1) * 128], ident_b),
                 reads=[Txh[sl], Tconst], writes=[Tb[pbk]])
        for kc in range(8):
            src_ps = psb[:, kc * 128:(kc + 1) * 128]
            dst = hT[:, kc, t * 128:(t + 1) * 128]
            if kc % 2 == 0:
                S.op("act", act(dst, src_ps, AF.Identity, scale=A1[:, c, kc:kc + 1], bias=SH[:, c, kc:kc + 1]),
                     reads=[Tb[pbk], Tmod], writes=[ThT[t]])
            else:
                S.op("dve", ts(dst, src_ps, A1[:, c, kc:kc + 1], SH[:, c, kc:kc + 1], ALU.mult, ALU.add),
                     reads=[Tb[pbk], Tmod], writes=[ThT[t]])

    cond = A.alloc([128, 8, 2], F32)
    scond = A.alloc([128, 8, 2], F32)
    Tcond = T("cond")
    Tscond = T("scond")
    wm = [A.alloc([128, 8, 512], F32) for _ in range(2)]
    Twm = [T("wm0"), T("wm1")]
    mrows = A.alloc([2, 3072], F32)
    Tmrows = T("mrows")
    bmod = A.alloc([2, 3072], F32)
    Tbmod = T("bmod")
    modc = A.alloc([128, 2, 16], F32)
    Tmodc = T("modc")
    lt1 = A.alloc([128, 4, 64], F32)
    Tlt = T("lamtmp")

    S.dma_load("sp", Tcond, [(cond, d_cond)])
    S.dma_load("sp", Tbmod, [(bmod, d_bmod)])
    S.op("act", act(scond, cond, AF.Silu), reads=[Tcond], writes=[Tscond])
    ph1_A(0)
    ph1_A(1)
    wmv = d_wmod.rearrange("(kc p) n -> p kc n", p=128)
    for j in range(6):
        sl = j % 2
        S.dma_load("sp", Twm[sl], [(wm[sl], wmv[:, :, j * 512:(j + 1) * 512])])
        for kc in range(8):
            S.op("pe", mm(PS[0:2, sl * 512:(sl + 1) * 512], scond[:, kc, :], wm[sl][:, kc, :], kc == 0, kc == 7),
                 reads=[Tscond, Twm[sl]], writes=[Tb[sl]])
        S.op("dve", tt(mrows[:, j * 512:(j + 1) * 512], PS[0:2, sl * 512:(sl + 1) * 512],
                       bmod[:, j * 512:(j + 1) * 512], ALU.add), reads=[Tb[sl], Tbmod], writes=[Tmrows])
    S.dma_store("sp", Tmrows, [(d_mscr, mrows)], dram_t=Tmscr)
    S.dma_load("sp", Tmodc, [(modc[:, c, :], d_mscr[c, 0:2048].rearrange("(j p) -> p j", p=128)) for c in range(2)],
               reads=[Tmscr])
    for c in range(2):
        S.op("dve", stt(A1[:, c, :], modc[:, c, 8:16], 1.0, ngcol, ALU.add, ALU.mult), reads=[Tmodc, Tcst], writes=[Tmod])
        S.op("dve", cp(SH[:, c, :], modc[:, c, 0:8]), reads=[Tmodc], writes=[Tmod])
    S.op("dve", tt(lt1[:, 0, :], lamv[:, 0, :], lamv[:, 1, :], ALU.mult), reads=[Tcst], writes=[Tlt])
    S.op("dve", tt(lt1[:, 1, :], lamv[:, 2, :], lamv[:, 3, :], ALU.mult), reads=[Tcst], writes=[Tlt])
    S.op("dve", lambda e: e.reduce_sum(out=lt1[:, 2, 0:2], in_=lt1[:, 0:2, :], axis=mybir.AxisListType.X), reads=[Tlt], writes=[Tlt])
    S.op("act", act(lt1[:, 3, 0:2], lt1[:, 2, 0:2], AF.Exp), reads=[Tlt], writes=[Tlt])
    S.op("dve", tt(neglam, lt1[:, 3, 1:2], lt1[:, 3, 0:1], ALU.subtract), reads=[Tlt], writes=[Tsm])
    S.op("dve", ts(neglam, neglam, -0.2, None, ALU.add), reads=[Tsm], writes=[Tsm])
    S.op("dve", ts(sublnc, subln, 0.8, None, ALU.mult), reads=[Tcst], writes=[Tsm])
    S.op("act", act(negA_s, alog_s, AF.Exp), reads=[Tcst], writes=[Tsm])
    S.op("act", act(negA_c, alog_c, AF.Exp), reads=[Tcst], writes=[Tsm])
    S.op("dve", ts(negA_s, negA_s, -1.0, None, ALU.mult), reads=[Tsm], writes=[Tsm])
    S.op("dve", ts(negA_c, negA_c, -1.0, None, ALU.mult), reads=[Tsm], writes=[Tsm])

    chk(0)
    for t in range(NTILE):
        if t + 2 < NTILE:
            ph1_A(t + 2)
        ph1_B(t)
    S.barrier()
    A.release(m0)

    chk(1)
    def hT_st(st):
        return [ThT[4 * st + i] for i in range(4)]

    m2 = A.mark()
    dtv = A.alloc([128, NTILE, 32], F32)
    av_o = A.alloc([128, 20, 32], F32)
    acum_o = A.alloc([128, 20, 32], F32)
    dte = A.alloc([128, NTILE, 32], F32)
    dec = A.alloc([128, NTILE, 32], F32)
    Tdt = T("dtv")
    Tav = T("av")
    Tcum = [T("cum%d" % i) for i in range(5)]
    B_tok = A.alloc([128, NTILE, 256], BF16)
    TBtok = [T("Btok%d" % i) for i in range(9)]
    BT_own = A.alloc([128, 2, NOWN], BF16)
    CT_own = A.alloc([128, 2, NOWN], BF16)
    TBT = [T("BT0"), T("BT1")]
    TCT = [T("CT0"), T("CT1")]
    ctmp = [A.alloc([128, 1024], F32) for _ in range(2)]
    Tctmp = [T("ctmp0"), T("ctmp1")]
    m2b = A.mark()
    wbc = A.alloc([128, 8, 576], BF16)
    Twbc = T("wbc")
    braw = A.alloc([128, 4616], BF16)
    Tbraw = [T("braw%d" % i) for i in range(9)]
    bcc = A.alloc([128, 4616], BF16)
    Tbcc = [T("bcc%d" % i) for i in range(5)]
    dtmp = A.alloc([128, 8, 32], F32)
    Tdtmp = T("dtmp")
    av = A.alloc([128, NTILE, 32], F32)
    acum = A.alloc([128, NTILE, 32], F32)
    Tavall = T("av_all")
    Tcumall = [T("cumall%d" % i) for i in range(5)]

    S.dma_load("pool", Twbc, [(wbc, wallv[:, :, 0:576])])
    S.op("dve", ms(braw, 0.0), writes=Tbraw)
    chk(1.1)

    def st_regions(c0, c1, Tl):
        out = []
        for st in range(8):
            lo, hi = 2 + 512 * st, 2 + 512 * (st + 1)
            if c0 < hi and c1 > lo:
                out.append(Tl[st])
        if c1 > 4100:
            out.append(Tl[8])
        return out

    conv_k = [0]

    def conv_piece(raw, Traw, c0, c1, wc, blk, out_ap, Tout):
        k = conv_k[0] % 2
        conv_k[0] += 1
        n = c1 - c0
        tmp = ctmp[k][:, 0:n]
        rg = st_regions(c0 - 1, c1 + 1, Traw)
        S.op("dve", ts(tmp, raw[:, c0:c1], wc[:, blk, 1:2], cbias[:, blk:blk + 1], ALU.mult, ALU.add),
             reads=rg + [Tcst], writes=[Tctmp[k]])
        S.op("dve", stt(tmp, raw[:, c0 - 1:c1 - 1], wc[:, blk, 0:1], tmp, ALU.mult, ALU.add),
             reads=rg + [Tctmp[k], Tcst], writes=[Tctmp[k]])
        S.op("dve", stt(tmp, raw[:, c0 + 1:c1 + 1], wc[:, blk, 2:3], tmp, ALU.mult, ALU.add),
             reads=rg + [Tctmp[k], Tcst], writes=[Tctmp[k]])
        S.op("act", act(out_ap, tmp, AF.Silu), reads=[Tctmp[k]], writes=[Tout])

    pj = [0]

    def project_fm(wslab, Tw, c0, sts, raw, Traw, pbanks=(0, 1)):
        for st in sts:
            bk = pbanks[pj[0] % 2]
            pj[0] += 1
            for kc in range(8):
                S.op("pe", mm(bank(bk), wslab[:, kc, c0:c0 + 128], hT[:, kc, st * 512:(st + 1) * 512], kc == 0, kc == 7),
                     reads=[Tw] + hT_st(st), writes=[Tb[bk]])
            if st < 8:
                S.op("act", cp(raw[:, 2 + 512 * st:2 + 512 * (st + 1)], bank(bk)), reads=[Tb[bk]], writes=[Traw[st]])
            else:
                S.op("act", cp(raw[:, 4100:4356], bank(bk, 256)), reads=[Tb[bk]], writes=[Traw[8]])
                S.op("act", cp(raw[:, 4358:4614], bank(bk, 256, 256)), reads=[Tb[bk]], writes=[Traw[8]])

    def bcc_piece(t):
        return 4 if t >= 32 else t // 8

    for bi in range(4):
        isB = bi < 2
        sts = list(range(9)) if isB else [0, 1, 2, 3, 4, 8]
        project_fm(wbc, Twbc, bi * 128, sts, braw, Tbraw)
        if bi == 0:
            chk(1.2)
        cblk = 8 + bi
        if isB:
            for p in range(4):
                conv_piece(braw, Tbraw, 2 + 1024 * p, 2 + 1024 * (p + 1), cw_s, cblk, bcc[:, 2 + 1024 * p:2 + 1024 * (p + 1)], Tbcc[p])
            conv_piece(braw, Tbraw, 4100, 4614, cw_c, cblk, bcc[:, 4100:4614], Tbcc[4])
            if bi == 0:
                chk(1.3)
            for t4 in range(9):
                pbk = 2 if t4 % 2 == 0 else 7
                psb = bank(pbk, 256).bitcast(BF16)
                for i in range(4):
                    t = 4 * t4 + i
                    pc = pcol(t * 128)
                    S.op("pe", tr(psb[:, i * 128:(i + 1) * 128], bcc[:, pc:pc + 128], ident_b),
                         reads=[Tbcc[bcc_piece(t)], Tconst], writes=[Tb[pbk]])
                S.op("dve", cp(B_tok[:, 4 * t4:4 * t4 + 4, bi * 128:(bi + 1) * 128], psb.rearrange("p (a b) -> p a b", a=4)),
                     reads=[Tb[pbk]], writes=[TBtok[t4]])
            S.op("act", cp(BT_own[:, bi, 0:2048], bcc[:, 2:2050]), reads=[Tbcc[0], Tbcc[1]], writes=[TBT[bi]])
            S.op("act", cp(BT_own[:, bi, 2048:2304], bcc[:, 4100:4356]), reads=[Tbcc[4]], writes=[TBT[bi]])
            S.op("act", cp(BT_own[:, bi, 2304:2560], bcc[:, 4358:4614]), reads=[Tbcc[4]], writes=[TBT[bi]])
            if bi == 0:
                chk(1.4)
        else:
            ci = bi - 2
            conv_piece(braw, Tbraw, 2, 1026, cw_s, cblk, CT_own[:, ci, 0:1024], TCT[ci])
            conv_piece(braw, Tbraw, 1026, 2050, cw_s, cblk, CT_own[:, ci, 1024:2048], TCT[ci])
            conv_piece(braw, Tbraw, 4100, 4356, cw_c, cblk, CT_own[:, ci, 2048:2304], TCT[ci])
            conv_piece(braw, Tbraw, 4358, 4614, cw_c, cblk, CT_own[:, ci, 2304:2560], TCT[ci])

    chk(1.5)
    for t8 in range(5):
        tiles = list(range(t8 * 8, min(NTILE, t8 * 8 + 8)))
        bk = 3 + (t8 % 2)
        for i, t in enumerate(tiles):
            for kc in range(8):
                S.op("pe", mm(bank(bk, 64, i * 64), hT[:, kc, t * 128:(t + 1) * 128], wbc[:, kc, 512:576], kc == 0, kc == 7),
                     reads=[Twbc, ThT[t]], writes=[Tb[bk]])
        n = len(tiles)
        psv = bank(bk, n * 64).rearrange("p (a b) -> p a b", a=n)
        if t8 < 4:
            S.op("dve", tt(dtv[:, tiles[0]:tiles[0] + n, :], psv[:, :, 0:32], dtb_s.unsqueeze(1).broadcast_to([128, n, 32]), ALU.add),
                 reads=[Tb[bk], Tcst], writes=[Tdt])
        else:
            S.op("dve", tt(dtv[:, tiles[0]:tiles[0] + n, :], psv[:, :, 32:64], dtb_c.unsqueeze(1).broadcast_to([128, n, 32]), ALU.add),
                 reads=[Tb[bk], Tcst], writes=[Tdt])
    S.op("act", act(dtv, dtv, AF.Exp), reads=[Tdt], writes=[Tdt])
    S.op("act", act(dtv, dtv, AF.Ln, bias=1.0), reads=[Tdt], writes=[Tdt])
    S.op("dve", tt(av[:, 0:32, :], dtv[:, 0:32, :], negA_s.unsqueeze(1).broadcast_to([128, 32, 32]), ALU.mult), reads=[Tdt, Tsm], writes=[Tavall])
    S.op("dve", tt(av[:, 32:36, :], dtv[:, 32:36, :], negA_c.unsqueeze(1).broadcast_to([128, 4, 32]), ALU.mult), reads=[Tdt, Tsm], writes=[Tavall])
    S.op("pool", cp(av_o[:, 0:16, :], av[:, 0:16, :]), reads=[Tavall], writes=[Tav])
    S.op("pool", cp(av_o[:, 16:20, :], av[:, 32:36, :]), reads=[Tavall], writes=[Tav])
    chk(1.6)
    for t8 in range(5):
        tiles = list(range(t8 * 8, min(NTILE, t8 * 8 + 8)))
        bk = 5 + (t8 % 2)
        for i, t in enumerate(tiles):
            S.op("pe", mm(bank(bk, 32, i * 64), ones_f, av[:, t, :]), reads=[Tavall, Tconst], writes=[Tb[bk]])
            S.op("pe", mm(bank(bk, 16, i * 64 + 32), mask_f, av[:, t, 0:16]), reads=[Tavall, Tconst], writes=[Tb[bk]])
            S.op("pe", mm(bank(bk, 16, i * 64 + 48), mask_b, av[:, t, 16:32]), reads=[Tavall, Tconst], writes=[Tb[bk]])
        n = len(tiles)
        t0 = tiles[0]
        if t8 == 0:
            chk(1.7)
        psv = bank(bk, n * 64).rearrange("p (a b) -> p a b", a=n)
        S.op("dve", cp(acum[:, t0:t0 + n, :], psv[:, :, 32:64]), reads=[Tb[bk]], writes=[Tcumall[t8]])
        if t8 == 0:
            chk(1.71)
        S.op("dve", cp(dec[:, t0:t0 + n, :], psv[:, :, 0:32]), reads=[Tb[bk]], writes=[Tcum[t8]])
        S.op("act", act(dec[:, t0:t0 + n, :], dec[:, t0:t0 + n, :], AF.Exp), reads=[Tcum[t8]], writes=[Tcum[t8]])
        if t8 == 0:
            chk(1.72)
        S.op("dve", tt(dtmp[:, 0:n, :], psv[:, :, 0:32], acum[:, t0:t0 + n, :], ALU.subtract), reads=[Tb[bk], Tcumall[t8]], writes=[Tdtmp])
        if t8 == 0:
            chk(1.73)
        S.op("act", act(dte[:, t0:t0 + n, :], dtmp[:, 0:n, :], AF.Exp), reads=[Tdtmp], writes=[Tcum[t8]])
        if t8 == 0:
            chk(1.74)
        if t8 < 2:
            S.op("pool", cp(acum_o[:, t0:t0 + 8, :], acum[:, t0:t0 + 8, :]), reads=[Tcumall[t8]], writes=[Tcum[t8]])
        elif t8 == 4:
            S.op("pool", cp(acum_o[:, 16:20, :], acum[:, 32:36, :]), reads=[Tcumall[t8]], writes=[Tcum[t8]])
        if t8 == 0:
            chk(1.8)
    chk(1.9)
    S.barrier()
    A.release(m2b)

    def Tcum_of(t):
        return Tcum[t // 8]

    chk(2)
    m3 = A.mark()
    wsl0 = A.alloc([128, 8, 256], BF16)
    wsl = [wsl0, wsl0]
    Twsl0 = T("wsl0")
    Twsl = [Twsl0, Twsl0]
    xraw = A.alloc([128, 4616], BF16)
    Txraw = [T("xraw%d" % i) for i in range(9)]
    xsc = A.alloc([128, 4616], BF16)
    Txsc = [T("xsc%d" % i) for i in range(5)]
    xdt_b = A.alloc([128, NTILE, 128], BF16)
    Txdb = [T("xdb%d" % i) for i in range(9)]
    xdt_f = A.alloc([128, 20, 128], BF16)
    Txdf = [T("xdf%d" % i) for i in range(5)]
    szb = A.alloc([128, NOWN], BF16)
    Tsz = [T("sz%d" % i) for i in range(5)]
    YTb = A.alloc([128, NOWN], BF16)
    TYTb = T("YTb")
    Sbst = A.alloc([128, 20, 2, 64], BF16)
    TSbst = [T("Sbst%d" % i) for i in range(20)]
    Sf = A.alloc([128, 2, 64], F32)
    Sb = A.alloc([128, 2, 64], F32)
    TSf = T("Sf")
    TSb = T("Sb")
    Sfb = [A.alloc([128, 2, 64], BF16) for _ in range(2)]
    TSfb = [T("Sfb0"), T("Sfb1")]
    h0blk = A.alloc([64, 4, 64], F32)
    Th0 = T("h0blk")
    hst = A.alloc([64, 4, 64], F32)
    Thst = T("hst")
    am = [A.alloc([128, 4, 128], F32) for _ in range(2)]
    tmpS = am
    LT = [A.alloc([128, 4, 128], BF16) for _ in range(2)]
    Gm = LT
    Ee = [A.alloc([128, 4, 128], BF16) for _ in range(2)]
    Cd = Ee
    Bdf = [A.alloc([128, 2, 64], BF16) for _ in range(2)]
    Bdb = [A.alloc([128, 2, 64], BF16) for _ in range(2)]
    y1 = [A.alloc([128, 128], F32) for _ in range(2)]
    y2 = y1
    sqb = [A.alloc([128, 128], BF16) for _ in range(2)]
    Tam = [T("am0"), T("am1")]
    TtmpS = [T("tmpS0"), T("tmpS1")]
    TLT = [T("LT0"), T("LT1")]
    TG = TLT
    TE = [T("E0"), T("E1")]
    TCd = TE
    TBdf = [T("Bdf0"), T("Bdf1")]
    TBdb = [T("Bdb0"), T("Bdb1")]
    Ty1 = [T("y1_0"), T("y1_1")]
    Ty2 = Ty1
    Tsq = [T("sq0"), T("sq1")]
    TpR = [Tb[3], Tb[4]]
    TpCB = [Tb[0], Tb[1]]
    TpY = [Tb[5], Tb[6]]
    TpIf = [Tb[5], Tb[6]]
    TpIb = [Tb[5], Tb[6]]
    TpQ = Tb[7]
    TpTr = Tb[7]

    S.op("dve", ms(xraw, 0.0), writes=Txraw)
    S.dma_load("pool", Twsl[0], [(wsl[0], wallv[:, :, 576:576 + 256])])
    for j in range(8):
        sl = j % 2
        W = wsl[sl]
        Tw = Twsl[sl]
        g = j // 2
        gp = (g % 2) * 64
        bb = g // 2
        project_fm(W, Tw, 0, list(range(9)), xraw, Txraw)
        for oi, st in enumerate(OWN_STS):
            bk = pj[0] % 2
            pj[0] += 1
            for kc in range(8):
                S.op("pe", mm(bank(bk), W[:, kc, 128:256], hT[:, kc, st * 512:(st + 1) * 512], kc == 0, kc == 7),
                     reads=[Tw] + hT_st(st), writes=[Tb[bk]])
            S.op("act", act(szb[:, oi * 512:(oi + 1) * 512], bank(bk), AF.Silu), reads=[Tb[bk]], writes=[Tsz[oi]])
        if j + 1 < 8:
            S.dma_load("pool", Twsl[sl], [(wsl[sl], wallv[:, :, 576 + (j + 1) * 256:576 + (j + 2) * 256])])
        for p in range(4):
            conv_piece(xraw, Txraw, 2 + 1024 * p, 2 + 1024 * (p + 1), cw_s, j, xsc[:, 2 + 1024 * p:2 + 1024 * (p + 1)], Txsc[p])
        conv_piece(xraw, Txraw, 4100, 4614, cw_c, j, xsc[:, 4100:4614], Txsc[4])
        for t4 in range(9):
            psb = bank(2, 256).bitcast(BF16)
            for i in range(4):
                t = 4 * t4 + i
                pc = pcol(t * 128)
                S.op("pe", tr(psb[:, i * 128:(i + 1) * 128], xsc[:, pc:pc + 128], ident_b),
                     reads=[Txsc[bcc_piece(t)], Tconst], writes=[Tb[2]])
            psv = psb.rearrange("p (a h q) -> p a h q", a=4, h=2)
            t0 = 4 * t4
            S.op("dve", tt(xdt_b[:, t0:t0 + 4, :].rearrange("p a (h q) -> p a h q", h=2), psv,
                           dtv[:, t0:t0 + 4, 16 + 2 * j:16 + 2 * j + 2].unsqueeze(3).broadcast_to([128, 4, 2, 64]), ALU.mult),
                 reads=[Tb[2], Tdt], writes=[Txdb[t4]])
            if t0 in OWN_TILES:
                o0 = otile(t0)
                S.op("dve", tt(xdt_f[:, o0:o0 + 4, :].rearrange("p a (h q) -> p a h q", h=2), psv,
                               dtv[:, t0:t0 + 4, 2 * j:2 * j + 2].unsqueeze(3).broadcast_to([128, 4, 2, 64]), ALU.mult),
                     reads=[Tb[2], Tdt], writes=[Txdf[o0 // 4]])
        kcount = 0
        for (tile0, ntl, nown, is_s, ci) in SEQS:
            if is_s:
                S.dma_load("sp", Th0, [(h0blk[:, 2 * d:2 * d + 2, :],
                                        d_h0[d, 2 * j:2 * j + 2, :, :].rearrange("h p n -> p h n")) for d in range(2)])
                for i in range(4):
                    S.op("pe", mm(PS[gp:gp + 64, 7 * 512 + 256 + i * 64:7 * 512 + 256 + (i + 1) * 64], h0blk[:, i, :], ident_f[0:64, 0:64]),
                         reads=[Th0, Tconst], writes=[TpTr])
                trv = PS[gp:gp + 64, 7 * 512 + 256:7 * 512 + 512].rearrange("p (a b) -> p a b", a=4)
                S.op("dve", cp(Sf[gp:gp + 64, :, :], trv[:, 0:2, :]), reads=[TpTr], writes=[TSf])
                S.op("dve", cp(Sb[gp:gp + 64, :, :], trv[:, 2:4, :]), reads=[TpTr], writes=[TSb])
            else:
                S.op("dve", ms(Sf[gp:gp + 64, :, :], 0.0), writes=[TSf])
                S.op("dve", ms(Sb[gp:gp + 64, :, :], 0.0), writes=[TSb])
            last = tile0 + ntl - 1
            if last in OWN_TILES:
                S.op("act", cp(Sbst[gp:gp + 64, otile(last), :, :], Sb[gp:gp + 64, :, :]), reads=[TSb], writes=[TSbst[otile(last)]])
            bcs = []
            for c in range(last, tile0 - 1, -1):
                if (c - 1 >= tile0) or (not is_s):
                    bcs.append(c)

            def A_b(c, k):
                S.op("pool", tt(Bdb[k], B_tok[:, c, g * 64:(g + 1) * 64].unsqueeze(1).broadcast_to([128, 2, 64]),
                                dte[:, c, 16 + 2 * j:16 + 2 * j + 2].unsqueeze(2).broadcast_to([128, 2, 64]), ALU.mult),
                     reads=[TBtok[c // 4], Tcum_of(c)], writes=[TBdb[k]])
                pI = PS[gp:gp + 64, (5 + k) * 512 + 384:(5 + k) * 512 + 512]
                for h2 in range(2):
                    S.op("pe", mm(pI[:, h2 * 64:(h2 + 1) * 64], Bdb[k][:, h2, :], xdt_b[:, c, h2 * 64:(h2 + 1) * 64]),
                         reads=[TBdb[k], Txdb[c // 4]], writes=[TpIb[k]])

            def U_b(c, k):
                pI = PS[gp:gp + 64, (5 + k) * 512 + 384:(5 + k) * 512 + 512]
                for h2 in range(2):
                    ix = 16 + 2 * j + h2
                    S.op("dve", stt(Sb[gp:gp + 64, h2, :], Sb[gp:gp + 64, h2, :], dec[gp:gp + 64, c, ix:ix + 1],
                                    pI[:, h2 * 64:(h2 + 1) * 64], ALU.mult, ALU.add),
                         reads=[TSb, Tcum_of(c), TpIb[k]], writes=[TSb])
                if c - 1 >= tile0 and (c - 1) in OWN_TILES:
                    oc1 = otile(c - 1)
                    S.op("act", cp(Sbst[gp:gp + 64, oc1, :, :], Sb[gp:gp + 64, :, :]), reads=[TSb], writes=[TSbst[oc1]])

            if bcs:
                kb0 = kcount
                A_b(bcs[0], kb0 % 2)
                for bi_, c in enumerate(bcs):
                    if bi_ + 1 < len(bcs):
                        A_b(bcs[bi_ + 1], (kb0 + bi_ + 1) % 2)
                    U_b(c, (kb0 + bi_) % 2)
                kcount += len(bcs)

            fcs = list(range(tile0, tile0 + nown))

            def upd_needed(c):
                return (c + 1 < tile0 + nown) or (not is_s)

            def A_f(c, k):
                oc = otile(c)
                ocs = slice(oc * 128, (oc + 1) * 128)
                for i in range(4):
                    d, h2 = i // 2, i % 2
                    ix = d * 16 + 2 * j + h2
                    S.op("act", act(am[k][:, i, :], mask_f if d == 0 else mask_b, AF.Identity, scale=av_o[:, oc, ix:ix + 1]),
                         reads=[Tav, Tconst], writes=[Tam[k], TtmpS[k]])
                pR = bank(3 + k)
                S.op("pe", mm(pR, ones_f, am[k].rearrange("p a b -> p (a b)")), reads=[Tam[k], Tconst], writes=[TpR[k]])
                pCB = bank(k, 128, 0)
                S.op("pe", mm(pCB, BT_own[gp:gp + 64, bb, ocs], CT_own[gp:gp + 64, bb, ocs]), reads=[TBT[bb], TCT[bb]], writes=[TpCB[k]])
                if upd_needed(c):
                    S.op("pool", tt(Bdf[k], B_tok[:, c, g * 64:(g + 1) * 64].unsqueeze(1).broadcast_to([128, 2, 64]),
                                    dte[:, c, 2 * j:2 * j + 2].unsqueeze(2).broadcast_to([128, 2, 64]), ALU.mult),
                         reads=[TBtok[c // 4], Tcum_of(c)], writes=[TBdf[k]])
                    pI = PS[gp:gp + 64, (5 + k) * 512 + 256:(5 + k) * 512 + 384]
                    for h2 in range(2):
                        S.op("pe", mm(pI[:, h2 * 64:(h2 + 1) * 64], Bdf[k][:, h2, :], xdt_f[:, oc, h2 * 64:(h2 + 1) * 64]),
                             reads=[TBdf[k], Txdf[oc // 4]], writes=[TpIf[k]])
                for i in range(4):
                    d, h2 = i // 2, i % 2
                    ix = d * 16 + 2 * j + h2
                    S.op("dve", stt(tmpS[k][:, i, :], pR[:, i * 128:(i + 1) * 128], acum_o[:, oc, ix:ix + 1],
                                    neg_f if d == 0 else neg_b, ALU.subtract, ALU.add),
                         reads=[TpR[k], Tcum_of(c), Tconst], writes=[TtmpS[k]])
                S.op("act", act(LT[k], tmpS[k], AF.Exp), reads=[TtmpS[k]], writes=[TLT[k]])
                S.op("act", act(Ee[k][gp:gp + 64].rearrange("p a b -> p (a b)"), pR[gp:gp + 64, :], AF.Exp), reads=[TpR[k]], writes=[TE[k]])

            def A2_f(c, k):
                oc = otile(c)
                ocs = slice(oc * 128, (oc + 1) * 128)
                pCB = bank(k, 128, 0)
                S.op("dve", tt(Gm[k], LT[k], pCB.unsqueeze(1).broadcast_to([128, 4, 128]), ALU.mult),
                     reads=[TLT[k], TpCB[k]], writes=[TG[k]])
                S.op("dve", tt(Cd[k][gp:gp + 64], Ee[k][gp:gp + 64],
                                CT_own[gp:gp + 64, bb, ocs].unsqueeze(1).broadcast_to([64, 4, 128]), ALU.mult),
                     reads=[TE[k], TCT[bb]], writes=[TCd[k]])

            def B_f(c, k):
                oc = otile(c)
                ocs = slice(oc * 128, (oc + 1) * 128)
                pc = pcol(c * 128)
                pY = bank(5 + k, 128, 128)
                for h2 in range(2):
                    hs = slice(h2 * 64, (h2 + 1) * 64)
                    S.op("pe", mm(pY[hs, :], xdt_f[:, oc, hs], Gm[k][:, h2, :], True, False),
                         reads=[Txdf[oc // 4], TG[k]], writes=[TpY[k]])
                    S.op("pe", mm(pY[hs, :], xdt_b[:, c, hs], Gm[k][:, 2 + h2, :], False, False),
                         reads=[Txdb[c // 4], TG[k]], writes=[TpY[k]])
                    S.op("pe", mm(pY[hs, :], Sbst[gp:gp + 64, oc, h2, :], Cd[k][gp:gp + 64, 2 + h2, :], False, False),
                         reads=[TSbst[oc], TCd[k]], writes=[TpY[k]])
                    S.op("pe", mm(pY[hs, :], Sfb[k][gp:gp + 64, h2, :], Cd[k][gp:gp + 64, h2, :], False, True),
                         reads=[TSfb[k], TCd[k]], writes=[TpY[k]])
                if upd_needed(c):
                    pI = PS[gp:gp + 64, (5 + k) * 512 + 256:(5 + k) * 512 + 384]
                    for h2 in range(2):
                        ix = 2 * j + h2
                        S.op("dve", stt(Sf[gp:gp + 64, h2, :], Sf[gp:gp + 64, h2, :], dec[gp:gp + 64, c, ix:ix + 1],
                                        pI[:, h2 * 64:(h2 + 1) * 64], ALU.mult, ALU.add),
                             reads=[TSf, Tcum_of(c), TpIf[k]], writes=[TSf])
                    S.op("act", cp(Sfb[1 - k][gp:gp + 64, :, :], Sf[gp:gp + 64, :, :]), reads=[TSf], writes=[TSfb[1 - k]])
                S.op("dve", stt(y1[k], xsc[:, pc:pc + 128], dcol[:, j:j + 1], pY, ALU.mult, ALU.add),
                     reads=[Txsc[bcc_piece(c)], Tcst, TpY[k]], writes=[Ty1[k]])
                S.op("pool", tt(y2[k], y1[k], szb[:, ocs], ALU.mult), reads=[Ty1[k], Tsz[oc // 4]], writes=[Ty2[k]])
                S.op("act", act(YTb[:, ocs], y2[k], AF.Identity, scale=sgcol[:, j:j + 1]), reads=[Ty2[k], Tcst], writes=[TYTb])
                S.op("act", act(sqb[k], y2[k], AF.Square), reads=[Ty2[k]], writes=[Tsq[k]])
                pend_ssq.append((oc, k))

            def flush_ssq():
                while pend_ssq:
                    oc_, k_ = pend_ssq.pop(0)
                    S.op("pe", mm(PS[:, 7 * 512 + oc_:7 * 512 + oc_ + 1], sqb[k_], ones_b[:, 0:1]), reads=[Tsq[k_], Tconst], writes=[TpQ])

            pend_ssq = []
            kf0 = kcount
            S.op("act", cp(Sfb[kf0 % 2][gp:gp + 64, :, :], Sf[gp:gp + 64, :, :]), reads=[TSf], writes=[TSfb[kf0 % 2]])
            A_f(fcs[0], kf0 % 2)
            A2_f(fcs[0], kf0 % 2)
            for fi_, c in enumerate(fcs):
                if fi_ + 1 < len(fcs):
                    A_f(fcs[fi_ + 1], (kf0 + fi_ + 1) % 2)
                prev = list(pend_ssq)
                del pend_ssq[:]
                B_f(c, (kf0 + fi_) % 2)
                cur = list(pend_ssq)
                del pend_ssq[:]
                pend_ssq.extend(prev)
                flush_ssq()
                pend_ssq.extend(cur)
                if fi_ + 1 < len(fcs):
                    A2_f(fcs[fi_ + 1], (kf0 + fi_ + 1) % 2)
            flush_ssq()
            kcount += len(fcs)
            if not is_s:
                for i in range(4):
                    src = (Sf if i < 2 else Sb)[gp:gp + 64, i % 2, :]
                    S.op("pe", tr(PS[0:64, 7 * 512 + 256 + i * 64:7 * 512 + 256 + (i + 1) * 64], src, ident_f[gp:gp + 64, gp:gp + 64]),
                         reads=[TSf, TSb, Tconst], writes=[TpTr])
                S.op("dve", cp(hst.rearrange("p a b -> p (a b)"), PS[0:64, 7 * 512 + 256:7 * 512 + 512]), reads=[TpTr], writes=[Thst])
                S.dma_store("sp", Thst, [(d_ho[ci, d, 2 * j:2 * j + 2, :, :].rearrange("h p n -> p h n"),
                                          hst[:, 2 * d:2 * d + 2, :]) for d in range(2)])
        S.op("dve", tt(ssq_tok, ssq_tok, PS[:, 7 * 512:7 * 512 + 20], ALU.add), reads=[Tssqt, TpQ], writes=[Tssqt])
        S.dma_store("sp", TYTb, [(d_yt[8 + j], YTb)], dram_t=Tyt[8 + j])
    S.barrier()
    A.release(m2)

    chk(3)
    m4 = A.mark()
    cosT = A.alloc([128, 4096], F32)
    sinT = A.alloc([128, 4096], F32)
    Trope = T("rope")
    S.dma_load("sp", Trope, [(cosT, d_cos), (sinT, d_sin)])
    wat = [A.alloc([128, 8, 768], BF16) for _ in range(2)]
    Twat = [T("wat0"), T("wat1")]
    kT = A.alloc([128, 5120], BF16)
    TkT = [T("kT%d" % i) for i in range(10)]
    v_tok = A.alloc([128, 40, 128], BF16)
    Tv = [T("v%d" % i) for i in range(10)]
    qT = A.alloc([128, NOWN], BF16)
    TqT = [T("qT%d" % i) for i in range(5)]
    sgT = A.alloc([128, NOWN], BF16)
    Tsg = [T("sg%d" % i) for i in range(5)]
    YTa = A.alloc([128, NOWN], BF16)
    TYTa = T("YTa")
    kp_tok = A.alloc([128, 4, 128], BF16)
    Tkp = T("kp_tok")
    rt = [A.alloc([128, 512], F32) for _ in range(2)]
    ru = [A.alloc([128, 512], F32) for _ in range(2)]
    Trt = [T("rt0"), T("rt1")]
    Tru = [T("ru0"), T("ru1")]
    kvst = [A.alloc([128, 256], F32) for _ in range(2)]
    Tkvst = [T("kvst0"), T("kvst1")]
    PT = [A.alloc([128, 1024], BF16) for _ in range(2)]
    TPT = [T("PT0"), T("PT1")]
    kb = [A.alloc([128, 512], BF16) for _ in range(2)]
    Tkb = [T("kb0"), T("kb1")]
    Pm = A.alloc([128, 128], BF16)
    TPm = T("Pm")
    idv = ident_b.rearrange("p (g t i) -> p g t i", g=4, t=2)
    pmv = Pm.rearrange("p (g t i) -> p g t i", g=4, t=2)
    S.op("dve", cp(pmv[:, :, 0, :], idv[:, :, 1, :]), reads=[Tconst], writes=[TPm])
    S.op("dve", cp(pmv[:, :, 1, :], idv[:, :, 0, :]), reads=[Tconst], writes=[TPm])
    rr = A.alloc([128, 2, 512], F32)
    tta = A.alloc([128, 2, 512], F32)
    r0, r1, t0a, t1a = rr[:, 0, :], rr[:, 1, :], tta[:, 0, :], tta[:, 1, :]
    att = A.alloc([128, 512], F32)
    sqa = A.alloc([128, 512], BF16)
    rsd = A.alloc([128, 512], F32)
    o1 = A.alloc([128, 512], F32)
    Tr0, Tr1, Tt0, Tt1, Tatt, Tsqa, Trsd, To1 = [T(n) for n in ("r0", "r1", "t0a", "t1a", "att", "sqa", "rsd", "o1")]
    TpO = [Tb[4], Tb[5]]
    TpL = [Tb[6], Tb[7]]

    ckv = d_ck.rearrange("(t p) h f -> p t h f", p=128)
    cvv = d_cv.rearrange("(t p) h f -> p t h f", p=128)
    ABASE = 576 + 2048
    S.dma_load("pool", Twat[0], [(wat[0], wallv[:, :, ABASE:ABASE + 768])])
    rk = [0]
    for h in range(8):
        sl = h % 2
        W = wat[sl]
        Tw = Twat[sl]
        if h + 1 < 8:
            S.dma_load("pool", Twat[1 - sl], [(wat[1 - sl], wallv[:, :, ABASE + (h + 1) * 768:ABASE + (h + 2) * 768])])
        S.dma_load("pool", Tkp, [(kp_tok, ckv[:, :, h, :])])
        S.dma_load("pool", Tv[8], [(v_tok[:, 32:36, :], cvv[:, :, h, :])])
        def proj_rope(col0, items):
            n_it = len(items)

            def stA(idx):
                st, dest, Td, is_rope = items[idx]
                b0 = 2 * (idx % 2)
                for kc in range(8):
                    S.op("pe", mm(bank(b0), W[:, kc, col0:col0 + 128], hT[:, kc, st * 512:(st + 1) * 512], kc == 0, kc == 7),
                         reads=[Tw] + hT_st(st), writes=[Tb[b0]])
                if is_rope:
                    S.op("act", cp(kb[idx % 2], bank(b0)), reads=[Tb[b0]], writes=[Tkb[idx % 2]])

            def stB(idx):
                st, dest, Td, is_rope = items[idx]
                b0 = 2 * (idx % 2)
                if is_rope:
                    S.op("pe", mm(bank(b0 + 1), Pm, kb[idx % 2]), reads=[TPm, Tkb[idx % 2]], writes=[Tb[b0 + 1]])
                    k = rk[0] % 2
                    rk[0] += 1
                    S.op("dve", tt(rt[k], bank(b0), cosT[:, st * 512:(st + 1) * 512], ALU.mult), reads=[Tb[b0], Trope, Tkb[idx % 2]], writes=[Trt[k]])
                    S.op("dve", tt(ru[k], bank(b0 + 1), sinT[:, st * 512:(st + 1) * 512], ALU.mult), reads=[Tb[b0 + 1], Trope], writes=[Tru[k]])
                    S.op("dve", tt(dest, rt[k], ru[k], ALU.add), reads=[Trt[k], Tru[k]], writes=[Td])
                else:
                    S.op("act", cp(dest, bank(b0)), reads=[Tb[b0]], writes=[Td])

            stA(0)
            for idx in range(n_it):
                if idx + 1 < n_it:
                    stA(idx + 1)
                stB(idx)

        kitems = [(st, kT[:, st * 512:(st + 1) * 512], TkT[st], True) for st in range(8)]
        kitems.append((8, kT[:, 4608:5120], TkT[9], False))
        proj_rope(384, kitems)
        if h == 0:
            chk(3.1)
        psb = PS[:, 2 * 512:2 * 512 + 256].bitcast(BF16)
        for i in range(4):
            S.op("pe", tr(psb[:, i * 128:(i + 1) * 128], kp_tok[:, i, :], ident_b), reads=[Tkp, Tconst], writes=[Tb[2]])
        S.op("act", cp(kT[:, 4096:4608], psb), reads=[Tb[2]], writes=[TkT[8]])
        if h == 0:
            chk(3.2)
        for t4 in range(9):
            bk = 2 + (t4 % 2)
            isc = t4 == 8
            ncol = 256 if isc else 128
            for i in range(4):
                t = 4 * t4 + i
                for kc in range(8):
                    if isc:
                        S.op("pe", mm(bank(2 + i // 2, 256, (i % 2) * 256),
                                      hT[:, kc, t * 128:(t + 1) * 128], W[:, kc, 384:640], kc == 0, kc == 7),
                             reads=[Tw, ThT[t]], writes=[Tb[2 + i // 2]])
                    else:
                        S.op("pe", mm(bank(bk, 128, i * 128), hT[:, kc, t * 128:(t + 1) * 128], W[:, kc, 512:640], kc == 0, kc == 7),
                             reads=[Tw, ThT[t]], writes=[Tb[bk]])
            if not isc:
                S.op("act", cp(v_tok[:, 4 * t4:4 * t4 + 4, :], bank(bk).rearrange("p (a b) -> p a b", a=4)), reads=[Tb[bk]], writes=[Tv[t4]])
            else:
                for i in range(4):
                    t = 32 + i
                    src = bank(2 + i // 2, 256, (i % 2) * 256)
                    kk = i % 2
                    S.op("dve", cp(kvst[kk], src), reads=[Tb[2 + i // 2]], writes=[Tkvst[kk]])
                    S.op("act", cp(v_tok[:, 36 + i, :], kvst[kk][:, 128:256]), reads=[Tkvst[kk]], writes=[Tv[9]])
                    S.dma_store("sp", Tkvst[kk], [(d_ko[i * 128:(i + 1) * 128, h, :], kvst[kk][:, 0:128]),
                                                  (d_vo[i * 128:(i + 1) * 128, h, :], kvst[kk][:, 128:256])])
        if h == 0:
            chk(3.3)
        qitems = [(st, qT[:, oi * 512:(oi + 1) * 512], TqT[oi], st < 8) for oi, st in enumerate(OWN_STS)]
        proj_rope(0, qitems)
        for oi, st in enumerate(OWN_STS):
            gb = 4 + (oi % 2)
            pg = bank(gb)
            for kc in range(8):
                S.op("pe", mm(pg, W[:, kc, 640:768], hT[:, kc, st * 512:(st + 1) * 512], kc == 0, kc == 7),
                     reads=[Tw] + hT_st(st), writes=[Tb[gb]])
            S.op("act", act(sgT[:, oi * 512:(oi + 1) * 512], pg, AF.Silu), reads=[Tb[gb]], writes=[Tsg[oi]])
        if h == 0:
            chk(3.4)
        jobs = []
        for qb in range(4):
            keys = [(kt * 128, kt, TkT[kt // 4], Tv[kt // 4]) for kt in range(36)]
            jobs.append((qb * 512, 512, keys, TqT[qb], Tsg[qb]))
        for ci in range(2):
            keys = [(4608 + ci * 256 + kk * 128, 36 + 2 * ci + kk, TkT[9], Tv[9]) for kk in range(2)]
            jobs.append((2048 + ci * 256, 256, keys, TqT[4], Tsg[4]))
        flat = []
        for ji, (q0, N, keys, Tq, Tsgq) in enumerate(jobs):
            for ki in range(len(keys)):
                flat.append((ji, ki))

        def emit_qk_exp(g):
            ji, ki = flat[g]
            q0, N, keys, Tq, Tsgq = jobs[ji]
            kc0, vt, Tk_, Tv_ = keys[ki]
            sb_ = g % 2
            pS = PS[:, sb_ * 1024:(sb_ + 1) * 1024]
            pSv = pS.rearrange("p (c n) -> p c n", c=2)
            for cpn in range(2):
                ps_ = slice(cpn * 64, (cpn + 1) * 64)
                S.op("pe", mm(pSv[:, cpn, 0:N], kT[ps_, kc0:kc0 + 128], qT[ps_, q0:q0 + N]),
                     reads=[Tk_, Tq], writes=[Tb[2 * sb_ + cpn]])
            PTv = PT[sb_].rearrange("p (c n) -> p c n", c=2)
            if N == 512:
                S.op("act", act(PT[sb_], pS, AF.Exp, scale=0.125), reads=[Tb[2 * sb_], Tb[2 * sb_ + 1]], writes=[TPT[sb_]])
            else:
                S.op("act", act(PTv[:, :, 0:N], pSv[:, :, 0:N], AF.Exp, scale=0.125), reads=[Tb[2 * sb_], Tb[2 * sb_ + 1]], writes=[TPT[sb_]])

        def emit_pv(g):
            ji, ki = flat[g]
            q0, N, keys, Tq, Tsgq = jobs[ji]
            kc0, vt, Tk_, Tv_ = keys[ki]
            sb_ = g % 2
            PTv = PT[sb_].rearrange("p (c n) -> p c n", c=2)
            pO = [bank(4, N), bank(5, N)]
            pL = [bank(6, N), bank(7, N)]
            first, lastk = ki == 0, ki == len(keys) - 1
            for cpn in range(2):
                S.op("pe", mm(pO[cpn], v_tok[:, vt, :], PTv[:, cpn, 0:N], first, lastk), reads=[Tv_, TPT[sb_]], writes=[TpO[cpn]])
                S.op("pe", mm(pL[cpn], ones_b, PTv[:, cpn, 0:N], first, lastk), reads=[Tconst, TPT[sb_]], writes=[TpL[cpn]])
            if not lastk:
                return
            qs = slice(q0, q0 + N)
            pLv = PS[:, 6 * 512:8 * 512].rearrange("p (c n) -> p c n", c=2)[:, :, 0:N]
            pOv = PS[:, 4 * 512:6 * 512].rearrange("p (c n) -> p c n", c=2)[:, :, 0:N]
            S.op("act", act(rr[:, :, 0:N], pLv, AF.Ln), reads=[TpL[0], TpL[1]], writes=[Tr0, Tr1])
            S.op("act", act(rr[:, :, 0:N], rr[:, :, 0:N], AF.Exp, scale=-1.0), reads=[Tr0, Tr1], writes=[Tr0, Tr1])
            S.op("dve", tt(tta[:, :, 0:N], pOv, rr[:, :, 0:N], ALU.mult), reads=[TpO[0], TpO[1], Tr0, Tr1], writes=[Tt0, Tt1])
            S.op("dve", stt(att[:, 0:N], t1a[:, 0:N], neglam[:, 0:1], t0a[:, 0:N], ALU.mult, ALU.add), reads=[Tt0, Tt1, Tsm], writes=[Tatt])
            S.op("dve", tt(sqa[:, 0:N], att[:, 0:N], att[:, 0:N], ALU.mult), reads=[Tatt], writes=[Tsqa])
            pN = bank(6, N)
            S.op("pe", mm(pN, ones_b, sqa[:, 0:N]), reads=[Tconst, Tsqa], writes=[TpL[0]])
            S.op("act", act(rsd[:, 0:N], pN, AF.Ln, scale=1.0 / 128, bias=EPS), reads=[TpL[0]], writes=[Trsd])
            S.op("act", act(rsd[:, 0:N], rsd[:, 0:N], AF.Exp, scale=-0.5), reads=[Trsd], writes=[Trsd])
            S.op("dve", stt(o1[:, 0:N], att[:, 0:N], sublnc[:, 0:1], rsd[:, 0:N], ALU.mult, ALU.mult), reads=[Tatt, Tsm, Trsd], writes=[To1])
            S.op("dve", tt(YTa[:, qs], o1[:, 0:N], sgT[:, qs], ALU.mult), reads=[To1, Tsgq], writes=[TYTa])

        G_ = len(flat)
        for g in range(G_ + 1):
            if g < G_:
                emit_qk_exp(g)
            if g >= 1:
                emit_pv(g - 1)
        S.dma_store("sp", TYTa, [(d_yt[h], YTa)], dram_t=Tyt[h])
        chk(3.7 + 0.01 * h)
    S.barrier()
    A.release(m2)

    chk(4)
    def h_alloc(shape, dt):
        return A.alloc(shape, dt)

    wo = h_alloc([128, 16, 1024], BF16)
    Two = T("wo")
    gbc = [h_alloc([128, 1024], F32) for _ in range(2)]
    Tgbc = T("gbc")
    fgb = h_alloc([128, 1024], F32)
    Tfg = T("fgb")
    ytl = [h_alloc([128, 16, 128], BF16) for _ in range(2)]
    Tytl = [T("ytl0"), T("ytl1")]
    xr = [h_alloc([128, 1024], F32) for _ in range(2)]
    Txr = [T("xr0"), T("xr1")]
    oa = h_alloc([128, 1024], F32)
    Toa = T("oa")
    ob = h_alloc([128, 1024], F32)
    Tob = T("ob")
    yo = [h_alloc([128, 1024], F32) for _ in range(2)]
    Tyo = [T("yo0"), T("yo1")]
    junk5 = h_alloc([128, 1024], BF16)
    Tj5 = T("junk5")
    rs5 = h_alloc([128, 20], F32)
    Trs5 = T("rs5")
    st5 = h_alloc([128, 3, 20], F32)
    Tst5 = [T("st5_%d" % i) for i in range(20)]

    S.dma_load("pool", Two, [(wo, d_wout.rearrange("(b p) n -> p b n", p=128))])
    S.dma_load("sp", Tgbc, [(gbc[c], d_mscr[c, 2048:3072].partition_broadcast(128)) for c in range(2)], reads=[Tmscr])
    S.dma_load("sp", Tfg, [(fgb, d_fg)])
    S.op("act", act(rs5, ssq_tok, AF.Ln, scale=1.0 / 1024, bias=EPS), reads=[Tssqt], writes=[Trs5])
    S.op("act", act(rs5, rs5, AF.Exp, scale=-0.5), reads=[Trs5], writes=[Trs5])
    ytv = d_yt.rearrange("b p t -> p b t")
    def ph5_A(ot):
        sl = ot % 2
        b0 = 4 * (ot % 2)
        S.dma_load("sp", Tytl[sl], [(ytl[sl], ytv[:, :, ot * 128:(ot + 1) * 128])], reads=Tyt)
        src = d_xs[ot * 128:(ot + 1) * 128, :] if ot < 16 else d_xc[(ot - 16) * 128:(ot - 15) * 128, :]
        S.dma_load("sp", Txr[sl], [(xr[sl], src)])
        for half in range(2):
            for b in range(8):
                S.op("pe", mm(bank(b0 + half), ytl[sl][:, b, :], wo[:, b, half * 512:(half + 1) * 512], b == 0, b == 7),
                     reads=[Tytl[sl], Two], writes=[Tb[b0 + half]])
            for b in range(8, 16):
                S.op("pe", mm(bank(b0 + 2 + half), ytl[sl][:, b, :], wo[:, b, half * 512:(half + 1) * 512], b == 8, b == 15),
                     reads=[Tytl[sl], Two], writes=[Tb[b0 + 2 + half]])

    def ph5_B(ot):
        sl = ot % 2
        b0 = 4 * (ot % 2)
        c = 0 if ot < 16 else 1
        S.op("act", cp(oa, PS[:, b0 * 512:b0 * 512 + 1024]), reads=[Tb[b0], Tb[b0 + 1]], writes=[Toa])
        S.op("dve", stt(ob, PS[:, (b0 + 2) * 512:(b0 + 2) * 512 + 1024], rs5[:, ot:ot + 1], oa, ALU.mult, ALU.add),
             reads=[Tb[b0 + 2], Tb[b0 + 3], Trs5, Toa], writes=[Tob])
        S.op("dve", tt(ob, ob, gbc[c], ALU.mult), reads=[Tob, Tgbc], writes=[Tob])
        S.op("dve", tt(ob, ob, xr[sl], ALU.add), reads=[Tob, Txr[sl]], writes=[Tob])
        S.op("act", act(junk5, ob, AF.Square, accum_out=st5[:, 0, ot:ot + 1]), reads=[Tob], writes=[Tj5, Tst5[ot]])
        S.op("act", act(st5[:, 1, ot:ot + 1], st5[:, 0, ot:ot + 1], AF.Ln, scale=1.0 / 1024, bias=EPS), reads=[Tst5[ot]], writes=[Tst5[ot]])
        S.op("act", act(st5[:, 2, ot:ot + 1], st5[:, 1, ot:ot + 1], AF.Exp, scale=-0.5), reads=[Tst5[ot]], writes=[Tst5[ot]])
        S.op("dve", stt(yo[sl], ob, st5[:, 2, ot:ot + 1], fgb, ALU.mult, ALU.mult), reads=[Tob, Tst5[ot], Tfg], writes=[Tyo[sl]])
        dst = d_ys[ot * 128:(ot + 1) * 128, :] if ot < 16 else d_yc[(ot - 16) * 128:(ot - 15) * 128, :]
        S.dma_store("sp", Tyo[sl], [(dst, yo[sl])])

    ph5_A(0)
    for ot in range(20):
        if ot + 1 < 20:
            ph5_A(ot + 1)
        ph5_B(ot)

    return


def _rope_tables():
    f32 = np.float32
    inv = (np.float32(10000.0) ** (-np.arange(16, dtype=f32) / np.float32(16))).astype(f32)
    t = np.arange(4096)
    row = (t // 64).astype(f32)
    col = (t % 64).astype(f32)
    f = np.arange(128)
    hf = (f % 64) // 32
    two = (f % 32) // 16
    i = f % 16
    pos = np.where(hf[:, None] == 0, row[None, :], col[None, :]).astype(f32)
    ang = (pos * inv[i][:, None]).astype(f32)
    cosT = np.cos(ang).astype(f32)
    sinT = (np.sin(ang) * np.where(two == 0, -1.0, 1.0)[:, None]).astype(f32)
    return cosT, sinT


def _prep_inputs(x_prompt, x_sample, cache_k, cache_v, state_ssd, c, c_ctx, w_mod, b_mod, norm_g, w_in,
                 lambda_q1, lambda_k1, lambda_q2, lambda_k2, subln_g, conv_w, conv_b, dt_bias, A_log,
                 D_skip, ssd_norm_g, w_out, final_g):
    f32 = np.float32
    A_ = lambda a: np.ascontiguousarray(np.asarray(a, dtype=f32))
    x_prompt, x_sample = A_(x_prompt), A_(x_sample)
    cache_k, cache_v, state_ssd = A_(cache_k), A_(cache_v), A_(state_ssd)
    c, c_ctx = A_(c), A_(c_ctx)
    w_in0 = A_(w_in)[0]
    wq, wk, wv, wg = w_in0[:, 0:1024], w_in0[:, 1024:2048], w_in0[:, 2048:3072], w_in0[:, 3072:4096]
    wz, wx = w_in0[:, 4096:5120], w_in0[:, 5120:6144]
    wB, wC, wdt = w_in0[:, 6144:6400], w_in0[:, 6400:6656], w_in0[:, 6656:6688]
    perm = np.arange(128) ^ 16
    cols = [wB, wC, wdt, wdt]
    for j in range(8):
        cols += [wx[:, j * 128:(j + 1) * 128], wz[:, j * 128:(j + 1) * 128]]
    for h in range(8):
        qh, kh = wq[:, h * 128:(h + 1) * 128], wk[:, h * 128:(h + 1) * 128]
        cols += [qh, qh[:, perm], kh[:, perm], kh, wv[:, h * 128:(h + 1) * 128], wg[:, h * 128:(h + 1) * 128]]
    wall0 = np.ascontiguousarray(np.concatenate(cols, axis=1))
    assert wall0.shape == (1024, NCOL)
    wall1 = wall0.copy()
    wall1[:, 512:528] = wdt[:, 16:32]
    wall1[:, 528:544] = wdt[:, 0:16]
    cosT, sinT = _rope_tables()
    cosF, sinF = np.ascontiguousarray(cosT[:, ::-1]), np.ascontiguousarray(sinT[:, ::-1])
    wmod0 = A_(w_mod)[0]
    bmod2 = np.ascontiguousarray(np.broadcast_to(A_(b_mod)[0][None, :], (2, 3072)))
    wout0 = A_(w_out)[0]
    fgbc = np.ascontiguousarray(np.broadcast_to(A_(final_g)[None, :], (128, 1024)))
    ng = A_(norm_g)[0]
    cw = A_(conv_w)[0]
    cb = A_(conv_b)[0]
    dtb = A_(dt_bias)[0]
    al = A_(A_log)[0]
    Dk = A_(D_skip)[0]
    sg = A_(ssd_norm_g)[0]
    sub = A_(subln_g)[0]
    lam = np.stack([A_(lambda_q1)[0], A_(lambda_k1)[0], A_(lambda_q2)[0], A_(lambda_k2)[0]], 0)

    def colmaj(v, nb):
        return v.reshape(nb, 128).T

    in_maps = []
    for core in range(NCORES):
        b, half = core // 2, core % 2
        cst = np.zeros((128, NCST), f32)
        cst[:, C_NG:C_NG + 8] = colmaj(ng, 8)
        cwn = np.stack([colmaj(cw[j], 12) for j in range(3)], axis=2)
        cws = cwn[:, :, ::-1] if half else cwn
        cst[:, C_CWS:C_CWS + 36] = cws.reshape(128, 36)
        cst[:, C_CWC:C_CWC + 36] = cwn.reshape(128, 36)
        cst[:, C_CB:C_CB + 12] = colmaj(cb, 12)
        dsel = [1, 0] if half else [0, 1]
        cst[:, C_DTBS:C_DTBS + 32] = dtb[dsel].reshape(32)[None, :]
        cst[:, C_DTBC:C_DTBC + 32] = dtb.reshape(32)[None, :]
        cst[:, C_ALS:C_ALS + 32] = al[dsel].reshape(32)[None, :]
        cst[:, C_ALC:C_ALC + 32] = al.reshape(32)[None, :]
        cst[:, C_D:C_D + 8] = np.repeat(Dk.reshape(8, 2).T, 64, axis=0)
        cst[:, C_SG:C_SG + 8] = colmaj(sg, 8)
        cst[:, C_SUB] = sub
        cst[:, C_LAM:C_LAM + 256] = lam.reshape(256)[None, :]
        xs = x_sample[b]
        if half:
            xs = np.ascontiguousarray(xs[::-1])
        cond2 = np.stack([colmaj(c[b], 8), colmaj(c_ctx, 8)], axis=2)
        h0 = state_ssd[b, 0]
        if half:
            h0 = h0[::-1]
        in_maps.append({
            "xs": np.ascontiguousarray(xs),
            "xc": np.ascontiguousarray(x_prompt[2 * core:2 * core + 2].reshape(512, 1024)),
            "cond2": np.ascontiguousarray(cond2),
            "wmod": wmod0,
            "bmod2": bmod2,
            "wall": wall1 if half else wall0,
            "wout": wout0,
            "cosT": cosF if half else cosT,
            "sinT": sinF if half else sinT,
            "ck": np.ascontiguousarray(cache_k[b, 0]),
            "cv": np.ascontiguousarray(cache_v[b, 0]),
            "h0": np.ascontiguousarray(h0),
            "cst": cst,
            "fgbc": fgbc,
        })
    return in_maps


_NC_CACHE = {}
_STATS = {}


def kernel(**inputs):
    in_maps = _prep_inputs(**inputs)
    if "nc" not in _NC_CACHE:
        _NC_CACHE["nc"] = build_program()
    nc = _NC_CACHE["nc"]
    res = run_bass_kernel_spmd(nc, in_maps, core_ids=list(range(NCORES)))
    f32 = np.float32
    y_prompt = np.zeros((16, 256, 1024), f32)
    y_sample = np.zeros((4, 4096, 1024), f32)
    new_k = np.zeros((16, 1, 256, 8, 128), f32)
    new_v = np.zeros((16, 1, 256, 8, 128), f32)
    new_h = np.zeros((16, 1, 2, 16, 64, 64), f32)
    for core in range(NCORES):
        r = res.results[core]
        b, half = core // 2, core % 2
        ys = np.asarray(r["ys"], dtype=f32)
        if half:
            y_sample[b, 2048:4096] = ys[::-1]
        else:
            y_sample[b, 0:2048] = ys
        y_prompt[2 * core:2 * core + 2] = np.asarray(r["yc"], dtype=f32).reshape(2, 256, 1024)
        new_k[2 * core:2 * core + 2, 0] = np.asarray(r["ko"], dtype=f32).reshape(2, 256, 8, 128)
        new_v[2 * core:2 * core + 2, 0] = np.asarray(r["vo"], dtype=f32).reshape(2, 256, 8, 128)
        new_h[2 * core:2 * core + 2, 0] = np.asarray(r["ho"], dtype=f32)
    return (y_prompt, y_sample, new_k, new_v, new_h)
```

```python
import math
import numpy as np
import concourse.bass as bass
import concourse.mybir as mybir
from concourse.bass_utils import run_bass_kernel_spmd

F32 = mybir.dt.float32
BF16 = mybir.dt.bfloat16
AF = mybir.ActivationFunctionType
ALU = mybir.AluOpType

EPS = 1e-6
NCORES = 8
NTILE = 36
NTOK = NTILE * 128
NOWN = 2560
NEGBIG = -30000.0
NCOL = 576 + 2048 + 6144
C_NG, C_CWS, C_CWC, C_CB, C_DTBS, C_DTBC, C_ALS, C_ALC, C_D, C_SG, C_SUB, C_LAM = (
    0, 8, 44, 80, 92, 124, 156, 188, 220, 228, 236, 237)
NCST = 237 + 256


class T:
    __slots__ = ("name", "w", "r", "ld", "st")

    def __init__(self, name):
        self.name = name
        self.w = None
        self.r = {}
        self.ld = None
        self.st = None


class Sched:
    ENG = ("pe", "act", "dve", "pool", "sp")

    def __init__(self, nc):
        self.nc = nc
        self.prog = {e: [] for e in self.ENG}
        self.cnt = {e: 0 for e in ("pe", "act", "dve", "pool")}
        self.waited = {e: {} for e in self.ENG}
        self.sems = {}
        self.nsem = 0
        self._stack = []
        self.dma_keys = {}
        for e in ("pe", "act", "dve", "pool"):
            self._newsem(e)
        self.final = []

    def _newsem(self, key):
        cm = self.nc.semaphore("s%d" % self.nsem)
        h = cm.__enter__()
        self._stack.append(cm)
        self.sems[key] = h
        self.nsem += 1
        return h

    def _deps(self, eng, reads, writes):
        deps = {}

        def add(k, c, kind):
            if k == eng:
                if eng == "pe" or kind == "waw":
                    return
            if deps.get(k, 0) < c:
                deps[k] = c
        for t in reads:
            if t.w is not None:
                add(t.w[0], t.w[1], "raw")
        for t in writes:
            if t.w is not None:
                add(t.w[0], t.w[1], "waw")
            for k, c in t.r.items():
                add(k, c, "war")
        return deps

    def _emit_waits(self, eng, deps):
        wd = self.waited[eng]
        for k, c in deps.items():
            if wd.get(k, 0) >= c:
                continue
            wd[k] = c
            h = self.sems[k]
            self.prog[eng].append(lambda e, h=h, c=c: e.wait_ge(h, c))

    def op(self, eng, fn, reads=(), writes=()):
        self._emit_waits(eng, self._deps(eng, reads, writes))
        self.cnt[eng] += 1
        c = self.cnt[eng]
        h = self.sems[eng]
        self.prog[eng].append(lambda e, fn=fn, h=h: fn(e).then_inc(h, 1))
        for t in writes:
            t.w = (eng, c)
            t.r = {}
        for t in reads:
            t.r[eng] = c

    def dma_load(self, q, tile, parts, reads=()):
        self._emit_waits(q, self._deps(q, reads, [tile]))
        if tile.ld is None:
            key = ("ld", self.nsem)
            self._newsem(key)
            tile.ld = [key, 0]
        key = tile.ld[0]
        h = self.sems[key]
        for (o, i) in parts:
            tile.ld[1] += 16
            self.prog[q].append(lambda e, o=o, i=i, h=h: e.dma_start(out=o, in_=i).then_inc(h, 16))
        tile.w = (key, tile.ld[1])
        tile.r = {}
        for t in reads:
            t.r[key] = tile.ld[1]
        self.dma_keys[key] = tile.ld[1]

    def dma_store(self, q, tile, parts, final=True, dram_t=None):
        wr = [dram_t] if dram_t is not None else []
        self._emit_waits(q, self._deps(q, [tile], wr))
        if tile.st is None:
            key = ("st", self.nsem)
            self._newsem(key)
            tile.st = [key, 0]
        key = tile.st[0]
        h = self.sems[key]
        for (o, i) in parts:
            tile.st[1] += 16
            self.prog[q].append(lambda e, o=o, i=i, h=h: e.dma_start(out=o, in_=i).then_inc(h, 16))
        tile.r[key] = tile.st[1]
        if dram_t is not None:
            dram_t.w = (key, tile.st[1])
            dram_t.r = {}
        self.dma_keys[key] = tile.st[1]

    def barrier(self):
        tgt = {e: c for e, c in self.cnt.items() if c > 0}
        tgt.update(self.dma_keys)
        for e in self.ENG:
            self._emit_waits(e, tgt)

    def finish(self):
        nc = self.nc
        self.barrier()
        prog = self.prog
        with nc.allow_non_contiguous_dma(reason="small strided constant/state transfers"), nc.Block() as block:
            @block.sync
            def _(e):
                for f in prog["sp"]:
                    f(e)

            @block.tensor
            def _(e):
                for f in prog["pe"]:
                    f(e)

            @block.scalar
            def _(e):
                for f in prog["act"]:
                    f(e)

            @block.vector
            def _(e):
                for f in prog["dve"]:
                    f(e)

            @block.gpsimd
            def _(e):
                for f in prog["pool"]:
                    f(e)
        for cm in reversed(self._stack):
            cm.__exit__(None, None, None)
        self._stack = []


def mm(out, lhsT, rhs, start=True, stop=True):
    return lambda e: e.matmul(out, lhsT=lhsT, rhs=rhs, start=start, stop=stop)


def tr(out, in_, ident):
    return lambda e: e.transpose(out=out, in_=in_, identity=ident)


def act(out, in_, func, scale=None, bias=None, accum_out=None):
    kw = {}
    if scale is not None:
        kw["scale"] = scale
    if bias is not None:
        kw["bias"] = bias
    if accum_out is not None:
        kw["accum_out"] = accum_out
    return lambda e: e.activation(out=out, in_=in_, func=func, **kw)


def tt(out, in0, in1, op):
    return lambda e: e.tensor_tensor(out=out, in0=in0, in1=in1, op=op)


def ts(out, in0, s1, s2, op0, op1=None):
    if op1 is None:
        return lambda e: e.tensor_scalar(out=out, in0=in0, scalar1=s1, scalar2=None, op0=op0)
    return lambda e: e.tensor_scalar(out=out, in0=in0, scalar1=s1, scalar2=s2, op0=op0, op1=op1)


def stt(out, in0, scalar, in1, op0, op1):
    return lambda e: e.scalar_tensor_tensor(out=out, in0=in0, scalar=scalar, in1=in1, op0=op0, op1=op1)


def cp(out, in_):
    return lambda e: (e.tensor_copy(out=out, in_=in_) if hasattr(e, "tensor_copy") else e.copy(out=out, in_=in_))


def ms(ap, v):
    return lambda e: e.memset(ap, v)


class Arena:
    def __init__(self, nc, words):
        self.cm = nc.sbuf_tensor("arena", [128, words], F32)
        self.t = self.cm.__enter__()
        self.words = words
        self.top = 0
        self.peak = 0

    def alloc(self, shape, dt):
        n = 1
        for s in shape[1:]:
            n *= s
        if dt == F32:
            w = n
        else:
            w = (n + 1) // 2
        assert self.top + w <= self.words, ("SBUF arena overflow", self.top, w, self.words)
        v = self.t[:, self.top:self.top + w]
        if dt != F32:
            v = v.bitcast(dt)[:, 0:n]
        self.top += w
        self.peak = max(self.peak, self.top)
        if len(shape) == 3:
            v = v.rearrange("p (a b) -> p a b", a=shape[1])
        elif len(shape) == 4:
            v = v.rearrange("p (a b c) -> p a b c", a=shape[1], b=shape[2])
        if shape[0] != 128:
            v = v[0:shape[0]]
        return v

    def mark(self):
        return self.top

    def release(self, m):
        self.top = m

    def close(self):
        self.cm.__exit__(None, None, None)


def pcol(t):
    if t < 4096:
        return t + 2
    if t < 4352:
        return t + 4
    return t + 6


def oidx(t):
    return t if t < 2048 else t - 2048


OWN_TILES = list(range(16)) + [32, 33, 34, 35]
OWN_STS = [0, 1, 2, 3, 8]
SEQS = [(0, 32, 16, True, -1), (32, 2, 2, False, 0), (34, 2, 2, False, 1)]


def otile(t):
    return t if t < 16 else t - 16


class _Stop(Exception):
    pass


def build_program(stop=None):
    nc = bass.Bass("TRN2", target_bir_lowering=False)
    env = {}
    try:
        _build_body(nc, stop, env)
    except _Stop:
        pass
    S, pcm, A = env["S"], env["pcm"], env["A"]
    _STATS.update(cnt=dict(S.cnt), nsem=S.nsem, peak_words=A.peak, nprog={k: len(v) for k, v in S.prog.items()})
    S.finish()
    pcm.__exit__(None, None, None)
    A.close()
    return nc


def _build_body(nc, stop, env):
    def chk(x):
        if stop is not None and abs(stop - x) < 1e-9:
            raise _Stop()

    def din(name, shape):
        return nc.dram_tensor(name, shape, F32, kind="ExternalInput").ap()

    def dout(name, shape):
        return nc.dram_tensor(name, shape, F32, kind="ExternalOutput").ap()

    d_xs = din("xs", [4096, 1024])
    d_xc = din("xc", [512, 1024])
    d_cond = din("cond2", [128, 8, 2])
    d_wmod = din("wmod", [1024, 3072])
    d_bmod = din("bmod2", [2, 3072])
    d_wall = din("wall", [1024, NCOL])
    d_wout = din("wout", [2048, 1024])
    d_cos = din("cosT", [128, 4096])
    d_sin = din("sinT", [128, 4096])
    d_ck = din("ck", [512, 8, 128])
    d_cv = din("cv", [512, 8, 128])
    d_h0 = din("h0", [2, 16, 64, 64])
    d_cst = din("cst", [128, NCST])
    d_fg = din("fgbc", [128, 1024])
    d_ys = dout("ys", [2048, 1024])
    d_yc = dout("yc", [512, 1024])
    d_ko = dout("ko", [512, 8, 128])
    d_vo = dout("vo", [512, 8, 128])
    d_ho = dout("ho", [2, 2, 16, 64, 64])
    d_mscr = nc.dram_tensor("mscr", [2, 3072], F32, kind="Internal").ap()
    d_yt = nc.dram_tensor("ytscr", [16, 128, NOWN], BF16, kind="Internal").ap()
    Tmscr = T("mscr")
    Tyt = [T("yt%d" % i) for i in range(16)]

    wallv = d_wall.rearrange("(kc p) n -> p kc n", p=128)

    S = Sched(nc)
    A = Arena(nc, 53200)
    pcm = nc.psum_tensor("PS", [128, 4096], F32)
    PS = pcm.__enter__()
    env.update(S=S, pcm=pcm, A=A)

    def bank(i, n=512, off=0):
        return PS[:, i * 512 + off:i * 512 + off + n]

    hT = A.alloc([128, 8, NTOK], BF16)
    ThT = [T("hT%d" % t) for t in range(NTILE)]
    cst = A.alloc([128, NCST], F32)
    Tcst = T("cst")
    ident_f = A.alloc([128, 128], F32)
    ident_b = A.alloc([128, 128], BF16)
    ones_f = A.alloc([128, 128], F32)
    ones_b = A.alloc([128, 128], BF16)
    mask_f = A.alloc([128, 128], F32)
    mask_b = A.alloc([128, 128], F32)
    neg_f = A.alloc([128, 128], F32)
    neg_b = A.alloc([128, 128], F32)
    Tconst = T("consts")
    A1 = A.alloc([128, 2, 8], F32)
    SH = A.alloc([128, 2, 8], F32)
    Tmod = T("mod")
    neglam = A.alloc([128, 1], F32)
    sublnc = A.alloc([128, 1], F32)
    negA_s = A.alloc([128, 32], F32)
    negA_c = A.alloc([128, 32], F32)
    Tsm = T("smallconsts")
    ssq_tok = A.alloc([128, 20], F32)
    Tssqt = T("ssq_tok")

    ngcol = cst[:, C_NG:C_NG + 8]
    cw_s = cst[:, C_CWS:C_CWS + 36].rearrange("p (a b) -> p a b", a=12)
    cw_c = cst[:, C_CWC:C_CWC + 36].rearrange("p (a b) -> p a b", a=12)
    cbias = cst[:, C_CB:C_CB + 12]
    dtb_s = cst[:, C_DTBS:C_DTBS + 32]
    dtb_c = cst[:, C_DTBC:C_DTBC + 32]
    alog_s = cst[:, C_ALS:C_ALS + 32]
    alog_c = cst[:, C_ALC:C_ALC + 32]
    dcol = cst[:, C_D:C_D + 8]
    sgcol = cst[:, C_SG:C_SG + 8]
    subln = cst[:, C_SUB:C_SUB + 1]
    lamv = cst[:, C_LAM:C_LAM + 256].rearrange("p (a b) -> p a b", a=4)

    Tb = [T("bank%d" % i) for i in range(8)]
    Tpsb = [T("psbA"), T("psbB")]

    S.dma_load("sp", Tcst, [(cst, d_cst)])
    S.op("pool", ms(ident_f, 1.0), writes=[Tconst])
    S.op("pool", lambda e: e.affine_select(out=ident_f, in_=ident_f, pattern=[[1, 128]], compare_op=ALU.is_equal,
                                           fill=0.0, base=0, channel_multiplier=-1), reads=[Tconst], writes=[Tconst])
    S.op("pool", cp(ident_b, ident_f), reads=[Tconst], writes=[Tconst])
    S.op("pool", ms(ones_f, 1.0), writes=[Tconst])
    S.op("pool", ms(ones_b, 1.0), writes=[Tconst])
    S.op("pool", ms(mask_f, 1.0), writes=[Tconst])
    S.op("pool", lambda e: e.affine_select(out=mask_f, in_=mask_f, pattern=[[1, 128]], compare_op=ALU.is_ge,
                                           fill=0.0, base=0, channel_multiplier=-1), reads=[Tconst], writes=[Tconst])
    S.op("pool", ms(mask_b, 1.0), writes=[Tconst])
    S.op("pool", lambda e: e.affine_select(out=mask_b, in_=mask_b, pattern=[[-1, 128]], compare_op=ALU.is_ge,
                                           fill=0.0, base=0, channel_multiplier=1), reads=[Tconst], writes=[Tconst])
    S.op("pool", ms(neg_f, 0.0), writes=[Tconst])
    S.op("pool", lambda e: e.affine_select(out=neg_f, in_=neg_f, pattern=[[1, 128]], compare_op=ALU.is_ge,
                                           fill=NEGBIG, base=0, channel_multiplier=-1), reads=[Tconst], writes=[Tconst])
    S.op("pool", ms(neg_b, 0.0), writes=[Tconst])
    S.op("pool", lambda e: e.affine_select(out=neg_b, in_=neg_b, pattern=[[-1, 128]], compare_op=ALU.is_ge,
                                           fill=NEGBIG, base=0, channel_multiplier=1), reads=[Tconst], writes=[Tconst])
    S.op("pool", ms(ssq_tok, 0.0), writes=[Tssqt])

    m0 = A.mark()
    m1 = A.mark()
    xt = [A.alloc([128, 1024], F32) for _ in range(4)]
    Txt = [T("xt%d" % i) for i in range(4)]
    xh = [A.alloc([128, 1024], BF16) for _ in range(4)]
    Txh = [T("xh%d" % i) for i in range(4)]
    junk = A.alloc([128, 1024], BF16)
    Tjunk = T("junk")
    st1 = A.alloc([128, 3, NTILE], F32)
    Tst1 = [T("st1_%d" % t) for t in range(NTILE)]
    def ph1_A(t):
        sl = t % 4
        src = d_xs[t * 128:(t + 1) * 128, :] if t < 32 else d_xc[(t - 32) * 128:(t - 31) * 128, :]
        S.dma_load("sp", Txt[sl], [(xt[sl], src)])
        S.op("act", act(junk, xt[sl], AF.Square, accum_out=st1[:, 0, t:t + 1]), reads=[Txt[sl]], writes=[Tjunk, Tst1[t]])
        S.op("act", act(st1[:, 1, t:t + 1], st1[:, 0, t:t + 1], AF.Ln, scale=1.0 / 1024, bias=EPS), reads=[Tst1[t]], writes=[Tst1[t]])
        S.op("act", act(st1[:, 2, t:t + 1], st1[:, 1, t:t + 1], AF.Exp, scale=-0.5), reads=[Tst1[t]], writes=[Tst1[t]])
        S.op("dve", ts(xh[sl], xt[sl], st1[:, 2, t:t + 1], None, ALU.mult), reads=[Txt[sl], Tst1[t]], writes=[Txh[sl]])

    def ph1_B(t):
        sl = t % 4
        c = 0 if t < 32 else 1
        pbk = t % 4
        psb = bank(pbk).bitcast(BF16)
        for kc in range(8):
            S.op("pe", tr(psb[:, kc * 128:(kc + 1) * 128], xh[sl][:, kc * 128:(kc + 1) * 128], ident_b),
                 reads=[Txh[sl], Tconst], writes=[Tb[pbk]])
        for kc in range(8):
            src_ps = psb[:, kc * 128:(kc + 1) * 128]
            dst = hT[:, kc, t * 128:(t + 1) * 128]
            if kc % 2 == 0:
                S.op("act", act(dst, src_ps, AF.Identity, scale=A1[:, c, kc:kc + 1], bias=SH[:, c, kc:kc + 1]),
                     reads=[Tb[pbk], Tmod], writes=[ThT[t]])
            else:
                S.op("dve", ts(dst, src_ps, A1[:, c, kc:kc + 1], SH[:, c, kc:kc + 1], ALU.mult, ALU.add),
                     reads=[Tb[pbk], Tmod], writes=[ThT[t]])

    cond = A.alloc([128, 8, 2], F32)
    scond = A.alloc([128, 8, 2], F32)
    Tcond = T("cond")
    Tscond = T("scond")
    wm = [A.alloc([128, 8, 512], F32) for _ in range(2)]
    Twm = [T("wm0"), T("wm1")]
    mrows = A.alloc([2, 3072], F32)
    Tmrows = T("mrows")
    bmod = A.alloc([2, 3072], F32)
    Tbmod = T("bmod")
    modc = A.alloc([128, 2, 16], F32)
    Tmodc = T("modc")
    lt1 = A.alloc([128, 4, 64], F32)
    Tlt = T("lamtmp")

    S.dma_load("sp", Tcond, [(cond, d_cond)])
    S.dma_load("sp", Tbmod, [(bmod, d_bmod)])
    S.op("act", act(scond, cond, AF.Silu), reads=[Tcond], writes=[Tscond])
    ph1_A(0)
    ph1_A(1)
    wmv = d_wmod.rearrange("(kc p) n -> p kc n", p=128)
    for j in range(6):
        sl = j % 2
        S.dma_load("sp", Twm[sl], [(wm[sl], wmv[:, :, j * 512:(j + 1) * 512])])
        for kc in range(8):
            S.op("pe", mm(PS[0:2, sl * 512:(sl + 1) * 512], scond[:, kc, :], wm[sl][:, kc, :], kc == 0, kc == 7),
                 reads=[Tscond, Twm[sl]], writes=[Tb[sl]])
        S.op("dve", tt(mrows[:, j * 512:(j + 1) * 512], PS[0:2, sl * 512:(sl + 1) * 512],
                       bmod[:, j * 512:(j + 1) * 512], ALU.add), reads=[Tb[sl], Tbmod], writes=[Tmrows])
    S.dma_store("sp", Tmrows, [(d_mscr, mrows)], dram_t=Tmscr)
    S.dma_load("sp", Tmodc, [(modc[:, c, :], d_mscr[c, 0:2048].rearrange("(j p) -> p j", p=128)) for c in range(2)],
               reads=[Tmscr])
    for c in range(2):
        S.op("dve", stt(A1[:, c, :], modc[:, c, 8:16], 1.0, ngcol, ALU.add, ALU.mult), reads=[Tmodc, Tcst], writes=[Tmod])
        S.op("dve", cp(SH[:, c, :], modc[:, c, 0:8]), reads=[Tmodc], writes=[Tmod])
    S.op("dve", tt(lt1[:, 0, :], lamv[:, 0, :], lamv[:, 1, :], ALU.mult), reads=[Tcst], writes=[Tlt])
    S.op("dve", tt(lt1[:, 1, :], lamv[:, 2, :], lamv[:, 3, :], ALU.mult), reads=[Tcst], writes=[Tlt])
    S.op("dve", lambda e: e.reduce_sum(out=lt1[:, 2, 0:2], in_=lt1[:, 0:2, :], axis=mybir.AxisListType.X), reads=[Tlt], writes=[Tlt])
    S.op("act", act(lt1[:, 3, 0:2], lt1[:, 2, 0:2], AF.Exp), reads=[Tlt], writes=[Tlt])
    S.op("dve", tt(neglam, lt1[:, 3, 1:2], lt1[:, 3, 0:1], ALU.subtract), reads=[Tlt], writes=[Tsm])
    S.op("dve", ts(neglam, neglam, -0.2, None, ALU.add), reads=[Tsm], writes=[Tsm])
    S.op("dve", ts(sublnc, subln, 0.8, None, ALU.mult), reads=[Tcst], writes=[Tsm])
    S.op("act", act(negA_s, alog_s, AF.Exp), reads=[Tcst], writes=[Tsm])
    S.op("act", act(negA_c, alog_c, AF.Exp), reads=[Tcst], writes=[Tsm])
    S.op("dve", ts(negA_s, negA_s, -1.0, None, ALU.mult), reads=[Tsm], writes=[Tsm])
    S.op("dve", ts(negA_c, negA_c, -1.0, None, ALU.mult), reads=[Tsm], writes=[Tsm])

    chk(0)
    for t in range(NTILE):
        if t + 2 < NTILE:
            ph1_A(t + 2)
        ph1_B(t)
    S.barrier()
    A.release(m0)

    chk(1)
    def hT_st(st):
        return [ThT[4 * st + i] for i in range(4)]

    m2 = A.mark()
    dtv = A.alloc([128, NTILE, 32], F32)
    av_o = A.alloc([128, 20, 32], F32)
    acum_o = A.alloc([128, 20, 32], F32)
    dte = A.alloc([128, NTILE, 32], F32)
    dec = A.alloc([128, NTILE, 32], F32)
    Tdt = T("dtv")
    Tav = T("av")
    Tcum = [T("cum%d" % i) for i in range(5)]
    B_tok = A.alloc([128, NTILE, 256], BF16)
    TBtok = [T("Btok%d" % i) for i in range(9)]
    BT_own = A.alloc([128, 2, NOWN], BF16)
    CT_own = A.alloc([128, 2, NOWN], BF16)
    TBT = [T("BT0"), T("BT1")]
    TCT = [T("CT0"), T("CT1")]
    ctmp = [A.alloc([128, 1024], F32) for _ in range(2)]
    Tctmp = [T("ctmp0"), T("ctmp1")]
    m2b = A.mark()
    wbc = A.alloc([128, 8, 576], BF16)
    Twbc = T("wbc")
    braw = A.alloc([128, 4616], BF16)
    Tbraw = [T("braw%d" % i) for i in range(9)]
    bcc = A.alloc([128, 4616], BF16)
    Tbcc = [T("bcc%d" % i) for i in range(5)]
    dtmp = A.alloc([128, 8, 32], F32)
    Tdtmp = T("dtmp")
    av = A.alloc([128, NTILE, 32], F32)
    acum = A.alloc([128, NTILE, 32], F32)
    Tavall = T("av_all")
    Tcumall = [T("cumall%d" % i) for i in range(5)]

    S.dma_load("pool", Twbc, [(wbc, wallv[:, :, 0:576])])
    S.op("dve", ms(braw, 0.0), writes=Tbraw)
    chk(1.1)

    def st_regions(c0, c1, Tl):
        out = []
        for st in range(8):
            lo, hi = 2 + 512 * st, 2 + 512 * (st + 1)
            if c0 < hi and c1 > lo:
                out.append(Tl[st])
        if c1 > 4100:
            out.append(Tl[8])
        return out

    conv_k = [0]

    def conv_piece(raw, Traw, c0, c1, wc, blk, out_ap, Tout):
        k = conv_k[0] % 2
        conv_k[0] += 1
        n = c1 - c0
        tmp = ctmp[k][:, 0:n]
        rg = st_regions(c0 - 1, c1 + 1, Traw)
        S.op("dve", ts(tmp, raw[:, c0:c1], wc[:, blk, 1:2], cbias[:, blk:blk + 1], ALU.mult, ALU.add),
             reads=rg + [Tcst], writes=[Tctmp[k]])
        S.op("dve", stt(tmp, raw[:, c0 - 1:c1 - 1], wc[:, blk, 0:1], tmp, ALU.mult, ALU.add),
             reads=rg + [Tctmp[k], Tcst], writes=[Tctmp[k]])
        S.op("dve", stt(tmp, raw[:, c0 + 1:c1 + 1], wc[:, blk, 2:3], tmp, ALU.mult, ALU.add),
             reads=rg + [Tctmp[k], Tcst], writes=[Tctmp[k]])
        S.op("act", act(out_ap, tmp, AF.Silu), reads=[Tctmp[k]], writes=[Tout])

    pj = [0]

    def project_fm(wslab, Tw, c0, sts, raw, Traw, pbanks=(0, 1)):
        for st in sts:
            bk = pbanks[pj[0] % 2]
            pj[0] += 1
            for kc in range(8):
                S.op("pe", mm(bank(bk), wslab[:, kc, c0:c0 + 128], hT[:, kc, st * 512:(st + 1) * 512], kc == 0, kc == 7),
                     reads=[Tw] + hT_st(st), writes=[Tb[bk]])
            if st < 8:
                S.op("act", cp(raw[:, 2 + 512 * st:2 + 512 * (st + 1)], bank(bk)), reads=[Tb[bk]], writes=[Traw[st]])
            else:
                S.op("act", cp(raw[:, 4100:4356], bank(bk, 256)), reads=[Tb[bk]], writes=[Traw[8]])
                S.op("act", cp(raw[:, 4358:4614], bank(bk, 256, 256)), reads=[Tb[bk]], writes=[Traw[8]])

    def bcc_piece(t):
        return 4 if t >= 32 else t // 8

    for bi in range(4):
        isB = bi < 2
        sts = list(range(9)) if isB else [0, 1, 2, 3, 4, 8]
        project_fm(wbc, Twbc, bi * 128, sts, braw, Tbraw)
        if bi == 0:
            chk(1.2)
        cblk = 8 + bi
        if isB:
            for p in range(4):
                conv_piece(braw, Tbraw, 2 + 1024 * p, 2 + 1024 * (p + 1), cw_s, cblk, bcc[:, 2 + 1024 * p:2 + 1024 * (p + 1)], Tbcc[p])
            conv_piece(braw, Tbraw, 4100, 4614, cw_c, cblk, bcc[:, 4100:4614], Tbcc[4])
            if bi == 0:
                chk(1.3)
            for t4 in range(9):
                pbk = 2 if t4 % 2 == 0 else 7
                psb = bank(pbk, 256).bitcast(BF16)
                for i in range(4):
                    t = 4 * t4 + i
                    pc = pcol(t * 128)
                    S.op("pe", tr(psb[:, i * 128:(i + 1) * 128], bcc[:, pc:pc + 128], ident_b),
                         reads=[Tbcc[bcc_piece(t)], Tconst], writes=[Tb[pbk]])
                S.op("dve", cp(B_tok[:, 4 * t4:4 * t4 + 4, bi * 128:(bi + 1) * 128], psb.rearrange("p (a b) -> p a b", a=4)),
                     reads=[Tb[pbk]], writes=[TBtok[t4]])
            S.op("act", cp(BT_own[:, bi, 0:2048], bcc[:, 2:2050]), reads=[Tbcc[0], Tbcc[1]], writes=[TBT[bi]])
            S.op("act", cp(BT_own[:, bi, 2048:2304], bcc[:, 4100:4356]), reads=[Tbcc[4]], writes=[TBT[bi]])
            S.op("act", cp(BT_own[:, bi, 2304:2560], bcc[:, 4358:4614]), reads=[Tbcc[4]], writes=[TBT[bi]])
            if bi == 0:
                chk(1.4)
        else:
            ci = bi - 2
            conv_piece(braw, Tbraw, 2, 1026, cw_s, cblk, CT_own[:, ci, 0:1024], TCT[ci])
            conv_piece(braw, Tbraw, 1026, 2050, cw_s, cblk, CT_own[:, ci, 1024:2048], TCT[ci])
            conv_piece(braw, Tbraw, 4100, 4356, cw_c, cblk, CT_own[:, ci, 2048:2304], TCT[ci])
            conv_piece(braw, Tbraw, 4358, 4614, cw_c, cblk, CT_own[:, ci, 2304:2560], TCT[ci])

    chk(1.5)
    for t8 in range(5):
        tiles = list(range(t8 * 8, min(NTILE, t8 * 8 + 8)))
        bk = 3 + (t8 % 2)
        for i, t in enumerate(tiles):
            for kc in range(8):
                S.op("pe", mm(bank(bk, 64, i * 64), hT[:, kc, t * 128:(t + 1) * 128], wbc[:, kc, 512:576], kc == 0, kc == 7),
                     reads=[Twbc, ThT[t]], writes=[Tb[bk]])
        n = len(tiles)
        psv = bank(bk, n * 64).rearrange("p (a b) -> p a b", a=n)
        if t8 < 4:
            S.op("dve", tt(dtv[:, tiles[0]:tiles[0] + n, :], psv[:, :, 0:32], dtb_s.unsqueeze(1).broadcast_to([128, n, 32]), ALU.add),
                 reads=[Tb[bk], Tcst], writes=[Tdt])
        else:
            S.op("dve", tt(dtv[:, tiles[0]:tiles[0] + n, :], psv[:, :, 32:64], dtb_c.unsqueeze(1).broadcast_to([128, n, 32]), ALU.add),
                 reads=[Tb[bk], Tcst], writes=[Tdt])
    S.op("act", act(dtv, dtv, AF.Exp), reads=[Tdt], writes=[Tdt])
    S.op("act", act(dtv, dtv, AF.Ln, bias=1.0), reads=[Tdt], writes=[Tdt])
    S.op("dve", tt(av[:, 0:32, :], dtv[:, 0:32, :], negA_s.unsqueeze(1).broadcast_to([128, 32, 32]), ALU.mult), reads=[Tdt, Tsm], writes=[Tavall])
    S.op("dve", tt(av[:, 32:36, :], dtv[:, 32:36, :], negA_c.unsqueeze(1).broadcast_to([128, 4, 32]), ALU.mult), reads=[Tdt, Tsm], writes=[Tavall])
    S.op("pool", cp(av_o[:, 0:16, :], av[:, 0:16, :]), reads=[Tavall], writes=[Tav])
    S.op("pool", cp(av_o[:, 16:20, :], av[:, 32:36, :]), reads=[Tavall], writes=[Tav])
    chk(1.6)
    for t8 in range(5):
        tiles = list(range(t8 * 8, min(NTILE, t8 * 8 + 8)))
        bk = 5 + (t8 % 2)
        for i, t in enumerate(tiles):
            S.op("pe", mm(bank(bk, 32, i * 64), ones_f, av[:, t, :]), reads=[Tavall, Tconst], writes=[Tb[bk]])
            S.op("pe", mm(bank(bk, 16, i * 64 + 32), mask_f, av[:, t, 0:16]), reads=[Tavall, Tconst], writes=[Tb[bk]])
            S.op("pe", mm(bank(bk, 16, i * 64 + 48), mask_b, av[:, t, 16:32]), reads=[Tavall, Tconst], writes=[Tb[bk]])
        n = len(tiles)
        t0 = tiles[0]
        if t8 == 0:
            chk(1.7)
        psv = bank(bk, n * 64).rearrange("p (a b) -> p a b", a=n)
        S.op("dve", cp(acum[:, t0:t0 + n, :], psv[:, :, 32:64]), reads=[Tb[bk]], writes=[Tcumall[t8]])
        if t8 == 0:
            chk(1.71)
        S.op("dve", cp(dec[:, t0:t0 + n, :], psv[:, :, 0:32]), reads=[Tb[bk]], writes=[Tcum[t8]])
        S.op("act", act(dec[:, t0:t0 + n, :], dec[:, t0:t0 + n, :], AF.Exp), reads=[Tcum[t8]], writes=[Tcum[t8]])
        if t8 == 0:
            chk(1.72)
        S.op("dve", tt(dtmp[:, 0:n, :], psv[:, :, 0:32], acum[:, t0:t0 + n, :], ALU.subtract), reads=[Tb[bk], Tcumall[t8]], writes=[Tdtmp])
        if t8 == 0:
            chk(1.73)
        S.op("act", act(dte[:, t0:t0 + n, :], dtmp[:, 0:n, :], AF.Exp), reads=[Tdtmp], writes=[Tcum[t8]])
        if t8 == 0:
            chk(1.74)
        if t8 < 2:
            S.op("pool", cp(acum_o[:, t0:t0 + 8, :], acum[:, t0:t0 + 8, :]), reads=[Tcumall[t8]], writes=[Tcum[t8]])
        elif t8 == 4:
            S.op("pool", cp(acum_o[:, 16:20, :], acum[:, 32:36, :]), reads=[Tcumall[t8]], writes=[Tcum[t8]])
        if t8 == 0:
            chk(1.8)
    chk(1.9)
    S.barrier()
    A.release(m2b)

    def Tcum_of(t):
        return Tcum[t // 8]

    chk(2)
    m3 = A.mark()
    wsl0 = A.alloc([128, 8, 256], BF16)
    wsl = [wsl0, wsl0]
    Twsl0 = T("wsl0")
    Twsl = [Twsl0, Twsl0]
    xraw = A.alloc([128, 4616], BF16)
    Txraw = [T("xraw%d" % i) for i in range(9)]
    xsc = A.alloc([128, 4616], BF16)
    Txsc = [T("xsc%d" % i) for i in range(5)]
    xdt_b = A.alloc([128, NTILE, 128], BF16)
    Txdb = [T("xdb%d" % i) for i in range(9)]
    xdt_f = A.alloc([128, 20, 128], BF16)
    Txdf = [T("xdf%d" % i) for i in range(5)]
    szb = A.alloc([128, NOWN], BF16)
    Tsz = [T("sz%d" % i) for i in range(5)]
    YTb = A.alloc([128, NOWN], BF16)
    TYTb = T("YTb")
    Sbst = A.alloc([128, 20, 2, 64], BF16)
    TSbst = [T("Sbst%d" % i) for i in range(20)]
    Sf = A.alloc([128, 2, 64], F32)
    Sb = A.alloc([128, 2, 64], F32)
    TSf = T("Sf")
    TSb = T("Sb")
    Sfb = [A.alloc([128, 2, 64], BF16) for _ in range(2)]
    TSfb = [T("Sfb0"), T("Sfb1")]
    h0blk = A.alloc([64, 4, 64], F32)
    Th0 = T("h0blk")
    hst = A.alloc([64, 4, 64], F32)
    Thst = T("hst")
    am = [A.alloc([128, 4, 128], F32) for _ in range(2)]
    tmpS = am
    LT = [A.alloc([128, 4, 128], BF16) for _ in range(2)]
    Gm = LT
    Ee = [A.alloc([128, 4, 128], BF16) for _ in range(2)]
    Cd = Ee
    Bdf = [A.alloc([128, 2, 64], BF16) for _ in range(2)]
    Bdb = [A.alloc([128, 2, 64], BF16) for _ in range(2)]
    y1 = [A.alloc([128, 128], F32) for _ in range(2)]
    y2 = y1
    sqb = [A.alloc([128, 128], BF16) for _ in range(2)]
    Tam = [T("am0"), T("am1")]
    TtmpS = [T("tmpS0"), T("tmpS1")]
    TLT = [T("LT0"), T("LT1")]
    TG = TLT
    TE = [T("E0"), T("E1")]
    TCd = TE
    TBdf = [T("Bdf0"), T("Bdf1")]
    TBdb = [T("Bdb0"), T("Bdb1")]
    Ty1 = [T("y1_0"), T("y1_1")]
    Ty2 = Ty1
    Tsq = [T("sq0"), T("sq1")]
    TpR = [Tb[3], Tb[4]]
    TpCB = [Tb[0], Tb[1]]
    TpY = [Tb[5], Tb[6]]
    TpIf = [Tb[5], Tb[6]]
    TpIb = [Tb[5], Tb[6]]
    TpQ = Tb[7]
    TpTr = Tb[7]

    S.op("dve", ms(xraw, 0.0), writes=Txraw)
    S.dma_load("pool", Twsl[0], [(wsl[0], wallv[:, :, 576:576 + 256])])
    for j in range(8):
        sl = j % 2
        W = wsl[sl]
        Tw = Twsl[sl]
        g = j // 2
        gp = (g % 2) * 64
        bb = g // 2
        project_fm(W, Tw, 0, list(range(9)), xraw, Txraw)
        for oi, st in enumerate(OWN_STS):
            bk = pj[0] % 2
            pj[0] += 1
            for kc in range(8):
                S.op("pe", mm(bank(bk), W[:, kc, 128:256], hT[:, kc, st * 512:(st + 1) * 512], kc == 0, kc == 7),
                     reads=[Tw] + hT_st(st), writes=[Tb[bk]])
            S.op("act", act(szb[:, oi * 512:(oi + 1) * 512], bank(bk), AF.Silu), reads=[Tb[bk]], writes=[Tsz[oi]])
        if j + 1 < 8:
            S.dma_load("pool", Twsl[sl], [(wsl[sl], wallv[:, :, 576 + (j + 1) * 256:576 + (j + 2) * 256])])
        for p in range(4):
            conv_piece(xraw, Txraw, 2 + 1024 * p, 2 + 1024 * (p + 1), cw_s, j, xsc[:, 2 + 1024 * p:2 + 1024 * (p + 1)], Txsc[p])
        conv_piece(xraw, Txraw, 4100, 4614, cw_c, j, xsc[:, 4100:4614], Txsc[4])
        for t4 in range(9):
            psb = bank(2, 256).bitcast(BF16)
            for i in range(4):
                t = 4 * t4 + i
                pc = pcol(t * 128)
                S.op("pe", tr(psb[:, i * 128:(i + 1) * 128], xsc[:, pc:pc + 128], ident_b),
                     reads=[Txsc[bcc_piece(t)], Tconst], writes=[Tb[2]])
            psv = psb.rearrange("p (a h q) -> p a h q", a=4, h=2)
            t0 = 4 * t4
            S.op("dve", tt(xdt_b[:, t0:t0 + 4, :].rearrange("p a (h q) -> p a h q", h=2), psv,
                           dtv[:, t0:t0 + 4, 16 + 2 * j:16 + 2 * j + 2].unsqueeze(3).broadcast_to([128, 4, 2, 64]), ALU.mult),
                 reads=[Tb[2], Tdt], writes=[Txdb[t4]])
            if t0 in OWN_TILES:
                o0 = otile(t0)
                S.op("dve", tt(xdt_f[:, o0:o0 + 4, :].rearrange("p a (h q) -> p a h q", h=2), psv,
                               dtv[:, t0:t0 + 4, 2 * j:2 * j + 2].unsqueeze(3).broadcast_to([128, 4, 2, 64]), ALU.mult),
                     reads=[Tb[2], Tdt], writes=[Txdf[o0 // 4]])
        kcount = 0
        for (tile0, ntl, nown, is_s, ci) in SEQS:
            if is_s:
                S.dma_load("sp", Th0, [(h0blk[:, 2 * d:2 * d + 2, :],
                                        d_h0[d, 2 * j:2 * j + 2, :, :].rearrange("h p n -> p h n")) for d in range(2)])
                for i in range(4):
                    S.op("pe", mm(PS[gp:gp + 64, 7 * 512 + 256 + i * 64:7 * 512 + 256 + (i + 1) * 64], h0blk[:, i, :], ident_f[0:64, 0:64]),
                         reads=[Th0, Tconst], writes=[TpTr])
                trv = PS[gp:gp + 64, 7 * 512 + 256:7 * 512 + 512].rearrange("p (a b) -> p a b", a=4)
                S.op("dve", cp(Sf[gp:gp + 64, :, :], trv[:, 0:2, :]), reads=[TpTr], writes=[TSf])
                S.op("dve", cp(Sb[gp:gp + 64, :, :], trv[:, 2:4, :]), reads=[TpTr], writes=[TSb])
            else:
                S.op("dve", ms(Sf[gp:gp + 64, :, :], 0.0), writes=[TSf])
                S.op("dve", ms(Sb[gp:gp + 64, :, :], 0.0), writes=[TSb])
            last = tile0 + ntl - 1
            if last in OWN_TILES:
                S.op("act", cp(Sbst[gp:gp + 64, otile(last), :, :], Sb[gp:gp + 64, :, :]), reads=[TSb], writes=[TSbst[otile(last)]])
            bcs = []
            for c in range(last, tile0 - 1, -1):
                if (c - 1 >= tile0) or (not is_s):
                    bcs.append(c)

            def A_b(c, k):
                S.op("pool", tt(Bdb[k], B_tok[:, c, g * 64:(g + 1) * 64].unsqueeze(1).broadcast_to([128, 2, 64]),
                                dte[:, c, 16 + 2 * j:16 + 2 * j + 2].unsqueeze(2).broadcast_to([128, 2, 64]), ALU.mult),
                     reads=[TBtok[c // 4], Tcum_of(c)], writes=[TBdb[k]])
                pI = PS[gp:gp + 64, (5 + k) * 512 + 384:(5 + k) * 512 + 512]
                for h2 in range(2):
                    S.op("pe", mm(pI[:, h2 * 64:(h2 + 1) * 64], Bdb[k][:, h2, :], xdt_b[:, c, h2 * 64:(h2 + 1) * 64]),
                         reads=[TBdb[k], Txdb[c // 4]], writes=[TpIb[k]])

            def U_b(c, k):
                pI = PS[gp:gp + 64, (5 + k) * 512 + 384:(5 + k) * 512 + 512]
                for h2 in range(2):
                    ix = 16 + 2 * j + h2
                    S.op("dve", stt(Sb[gp:gp + 64, h2, :], Sb[gp:gp + 64, h2, :], dec[gp:gp + 64, c, ix:ix + 1],
                                    pI[:, h2 * 64:(h2 + 1) * 64], ALU.mult, ALU.add),
                         reads=[TSb, Tcum_of(c), TpIb[k]], writes=[TSb])
                if c - 1 >= tile0 and (c - 1) in OWN_TILES:
                    oc1 = otile(c - 1)
                    S.op("act", cp(Sbst[gp:gp + 64, oc1, :, :], Sb[gp:gp + 64, :, :]), reads=[TSb], writes=[TSbst[oc1]])

            if bcs:
                kb0 = kcount
                A_b(bcs[0], kb0 % 2)
                for bi_, c in enumerate(bcs):
                    if bi_ + 1 < len(bcs):
                        A_b(bcs[bi_ + 1], (kb0 + bi_ + 1) % 2)
                    U_b(c, (kb0 + bi_) % 2)
                kcount += len(bcs)

            fcs = list(range(tile0, tile0 + nown))

            def upd_needed(c):
                return (c + 1 < tile0 + nown) or (not is_s)

            def A_f(c, k):
                oc = otile(c)
                ocs = slice(oc * 128, (oc + 1) * 128)
                for i in range(4):
                    d, h2 = i // 2, i % 2
                    ix = d * 16 + 2 * j + h2
                    S.op("act", act(am[k][:, i, :], mask_f if d == 0 else mask_b, AF.Identity, scale=av_o[:, oc, ix:ix + 1]),
                         reads=[Tav, Tconst], writes=[Tam[k], TtmpS[k]])
                pR = bank(3 + k)
                S.op("pe", mm(pR, ones_f, am[k].rearrange("p a b -> p (a b)")), reads=[Tam[k], Tconst], writes=[TpR[k]])
                pCB = bank(k, 128, 0)
                S.op("pe", mm(pCB, BT_own[gp:gp + 64, bb, ocs], CT_own[gp:gp + 64, bb, ocs]), reads=[TBT[bb], TCT[bb]], writes=[TpCB[k]])
                if upd_needed(c):
                    S.op("pool", tt(Bdf[k], B_tok[:, c, g * 64:(g + 1) * 64].unsqueeze(1).broadcast_to([128, 2, 64]),
                                    dte[:, c, 2 * j:2 * j + 2].unsqueeze(2).broadcast_to([128, 2, 64]), ALU.mult),
                         reads=[TBtok[c // 4], Tcum_of(c)], writes=[TBdf[k]])
                    pI = PS[gp:gp + 64, (5 + k) * 512 + 256:(5 + k) * 512 + 384]
                    for h2 in range(2):
                        S.op("pe", mm(pI[:, h2 * 64:(h2 + 1) * 64], Bdf[k][:, h2, :], xdt_f[:, oc, h2 * 64:(h2 + 1) * 64]),
                             reads=[TBdf[k], Txdf[oc // 4]], writes=[TpIf[k]])
                for i in range(4):
                    d, h2 = i // 2, i % 2
                    ix = d * 16 + 2 * j + h2
                    S.op("dve", stt(tmpS[k][:, i, :], pR[:, i * 128:(i + 1) * 128], acum_o[:, oc, ix:ix + 1],
                                    neg_f if d == 0 else neg_b, ALU.subtract, ALU.add),
                         reads=[TpR[k], Tcum_of(c), Tconst], writes=[TtmpS[k]])
                S.op("act", act(LT[k], tmpS[k], AF.Exp), reads=[TtmpS[k]], writes=[TLT[k]])
                S.op("act", act(Ee[k][gp:gp + 64].rearrange("p a b -> p (a b)"), pR[gp:gp + 64, :], AF.Exp), reads=[TpR[k]], writes=[TE[k]])

            def A2_f(c, k):
                oc = otile(c)
                ocs = slice(oc * 128, (oc + 1) * 128)
                pCB = bank(k, 128, 0)
                S.op("dve", tt(Gm[k], LT[k], pCB.unsqueeze(1).broadcast_to([128, 4, 128]), ALU.mult),
                     reads=[TLT[k], TpCB[k]], writes=[TG[k]])
                S.op("dve", tt(Cd[k][gp:gp + 64], Ee[k][gp:gp + 64],
                                CT_own[gp:gp + 64, bb, ocs].unsqueeze(1).broadcast_to([64, 4, 128]), ALU.mult),
                     reads=[TE[k], TCT[bb]], writes=[TCd[k]])

            def B_f(c, k):
                oc = otile(c)
                ocs = slice(oc * 128, (oc + 1) * 128)
                pc = pcol(c * 128)
                pY = bank(5 + k, 128, 128)
                for h2 in range(2):
                    hs = slice(h2 * 64, (h2 + 1) * 64)
                    S.op("pe", mm(pY[hs, :], xdt_f[:, oc, hs], Gm[k][:, h2, :], True, False),
                         reads=[Txdf[oc // 4], TG[k]], writes=[TpY[k]])
                    S.op("pe", mm(pY[hs, :], xdt_b[:, c, hs], Gm[k][:, 2 + h2, :], False, False),
                         reads=[Txdb[c // 4], TG[k]], writes=[TpY[k]])
                    S.op("pe", mm(pY[hs, :], Sbst[gp:gp + 64, oc, h2, :], Cd[k][gp:gp + 64, 2 + h2, :], False, False),
                         reads=[TSbst[oc], TCd[k]], writes=[TpY[k]])
                    S.op("pe", mm(pY[hs, :], Sfb[k][gp:gp + 64, h2, :], Cd[k][gp:gp + 64, h2, :], False, True),
                         reads=[TSfb[k], TCd[k]], writes=[TpY[k]])
                if upd_needed(c):
                    pI = PS[gp:gp + 64, (5 + k) * 512 + 256:(5 + k) * 512 + 384]
                    for h2 in range(2):
                        ix = 2 * j + h2
                        S.op("dve", stt(Sf[gp:gp + 64, h2, :], Sf[gp:gp + 64, h2, :], dec[gp:gp + 64, c, ix:ix + 1],
                                        pI[:, h2 * 64:(h2 + 1) * 64], ALU.mult, ALU.add),
                             reads=[TSf, Tcum_of(c), TpIf[k]], writes=[TSf])
                    S.op("act", cp(Sfb[1 - k][gp:gp + 64, :, :], Sf[gp:gp + 64, :, :]), reads=[TSf], writes=[TSfb[1 - k]])
                S.op("dve", stt(y1[k], xsc[:, pc:pc + 128], dcol[:, j:j + 1], pY, ALU.mult, ALU.add),
                     reads=[Txsc[bcc_piece(c)], Tcst, TpY[k]], writes=[Ty1[k]])
                S.op("pool", tt(y2[k], y1[k], szb[:, ocs], ALU.mult), reads=[Ty1[k], Tsz[oc // 4]], writes=[Ty2[k]])
                S.op("act", act(YTb[:, ocs], y2[k], AF.Identity, scale=sgcol[:, j:j + 1]), reads=[Ty2[k], Tcst], writes=[TYTb])
                S.op("act", act(sqb[k], y2[k], AF.Square), reads=[Ty2[k]], writes=[Tsq[k]])
                pend_ssq.append((oc, k))

            def flush_ssq():
                while pend_ssq:
                    oc_, k_ = pend_ssq.pop(0)
                    S.op("pe", mm(PS[:, 7 * 512 + oc_:7 * 512 + oc_ + 1], sqb[k_], ones_b[:, 0:1]), reads=[Tsq[k_], Tconst], writes=[TpQ])

            pend_ssq = []
            kf0 = kcount
            S.op("act", cp(Sfb[kf0 % 2][gp:gp + 64, :, :], Sf[gp:gp + 64, :, :]), reads=[TSf], writes=[TSfb[kf0 % 2]])
            A_f(fcs[0], kf0 % 2)
            A2_f(fcs[0], kf0 % 2)
            for fi_, c in enumerate(fcs):
                if fi_ + 1 < len(fcs):
                    A_f(fcs[fi_ + 1], (kf0 + fi_ + 1) % 2)
                prev = list(pend_ssq)
                del pend_ssq[:]
                B_f(c, (kf0 + fi_) % 2)
                cur = list(pend_ssq)
                del pend_ssq[:]
                pend_ssq.extend(prev)
                flush_ssq()
                pend_ssq.extend(cur)
                if fi_ + 1 < len(fcs):
                    A2_f(fcs[fi_ + 1], (kf0 + fi_ + 1) % 2)
            flush_ssq()
            kcount += len(fcs)
            if not is_s:
                for i in range(4):
                    src = (Sf if i < 2 else Sb)[gp:gp + 64, i % 2, :]
                    S.op("pe", tr(PS[0:64, 7 * 512 + 256 + i * 64:7 * 512 + 256 + (i + 1) * 64], src, ident_f[gp:gp + 64, gp:gp + 64]),
                         reads=[TSf, TSb, Tconst], writes=[TpTr])
                S.op("dve", cp(hst.rearrange("p a b -> p (a b)"), PS[0:64, 7 * 512 + 256:7 * 512 + 512]), reads=[TpTr], writes=[Thst])
                S.dma_store("sp", Thst, [(d_ho[ci, d, 2 * j:2 * j + 2, :, :].rearrange("h p n -> p h n"),
                                          hst[:, 2 * d:2 * d + 2, :]) for d in range(2)])
        S.op("dve", tt(ssq_tok, ssq_tok, PS[:, 7 * 512:7 * 512 + 20], ALU.add), reads=[Tssqt, TpQ], writes=[Tssqt])
        S.dma_store("sp", TYTb, [(d_yt[8 + j], YTb)], dram_t=Tyt[8 + j])
    S.barrier()
    A.release(m2)

    chk(3)
    m4 = A.mark()
    cosT = A.alloc([128, 4096], F32)
    sinT = A.alloc([128, 4096], F32)
    Trope = T("rope")
    S.dma_load("sp", Trope, [(cosT, d_cos), (sinT, d_sin)])
    wat = [A.alloc([128, 8, 768], BF16) for _ in range(2)]
    Twat = [T("wat0"), T("wat1")]
    kT = A.alloc([128, 5120], BF16)
    TkT = [T("kT%d" % i) for i in range(10)]
    v_tok = A.alloc([128, 40, 128], BF16)
    Tv = [T("v%d" % i) for i in range(10)]
    qT = A.alloc([128, NOWN], BF16)
    TqT = [T("qT%d" % i) for i in range(5)]
    sgT = A.alloc([128, NOWN], BF16)
    Tsg = [T("sg%d" % i) for i in range(5)]
    YTa = A.alloc([128, NOWN], BF16)
    TYTa = T("YTa")
    kp_tok = A.alloc([128, 4, 128], BF16)
    Tkp = T("kp_tok")
    rt = [A.alloc([128, 512], F32) for _ in range(2)]
    ru = [A.alloc([128, 512], F32) for _ in range(2)]
    Trt = [T("rt0"), T("rt1")]
    Tru = [T("ru0"), T("ru1")]
    kvst = [A.alloc([128, 256], F32) for _ in range(2)]
    Tkvst = [T("kvst0"), T("kvst1")]
    PT = [A.alloc([128, 1024], BF16) for _ in range(3)]
    TPT = [T("PT0"), T("PT1"), T("PT2")]
    kb = [A.alloc([128, 512], BF16) for _ in range(2)]
    Tkb = [T("kb0"), T("kb1")]
    Pm = A.alloc([128, 128], BF16)
    TPm = T("Pm")
    idv = ident_b.rearrange("p (g t i) -> p g t i", g=4, t=2)
    pmv = Pm.rearrange("p (g t i) -> p g t i", g=4, t=2)
    S.op("dve", cp(pmv[:, :, 0, :], idv[:, :, 1, :]), reads=[Tconst], writes=[TPm])
    S.op("dve", cp(pmv[:, :, 1, :], idv[:, :, 0, :]), reads=[Tconst], writes=[TPm])
    rr = A.alloc([128, 2, 512], F32)
    tta = A.alloc([128, 2, 512], F32)
    r0, r1, t0a, t1a = rr[:, 0, :], rr[:, 1, :], tta[:, 0, :], tta[:, 1, :]
    att = A.alloc([128, 512], F32)
    sqa = A.alloc([128, 512], BF16)
    rsd = A.alloc([128, 512], F32)
    o1 = A.alloc([128, 512], F32)
    Tr0, Tr1, Tt0, Tt1, Tatt, Tsqa, Trsd, To1 = [T(n) for n in ("r0", "r1", "t0a", "t1a", "att", "sqa", "rsd", "o1")]
    TpO = [Tb[4], Tb[5]]
    TpL = [Tb[6], Tb[7]]

    ckv = d_ck.rearrange("(t p) h f -> p t h f", p=128)
    cvv = d_cv.rearrange("(t p) h f -> p t h f", p=128)
    ABASE = 576 + 2048
    S.dma_load("pool", Twat[0], [(wat[0], wallv[:, :, ABASE:ABASE + 768])])
    rk = [0]
    for h in range(8):
        sl = h % 2
        W = wat[sl]
        Tw = Twat[sl]
        if h + 1 < 8:
            S.dma_load("pool", Twat[1 - sl], [(wat[1 - sl], wallv[:, :, ABASE + (h + 1) * 768:ABASE + (h + 2) * 768])])
        S.dma_load("pool", Tkp, [(kp_tok, ckv[:, :, h, :])])
        S.dma_load("pool", Tv[8], [(v_tok[:, 32:36, :], cvv[:, :, h, :])])
        def proj_rope(col0, items):
            n_it = len(items)

            def stA(idx):
                st, dest, Td, is_rope = items[idx]
                b0 = 2 * (idx % 2)
                for kc in range(8):
                    S.op("pe", mm(bank(b0), W[:, kc, col0:col0 + 128], hT[:, kc, st * 512:(st + 1) * 512], kc == 0, kc == 7),
                         reads=[Tw] + hT_st(st), writes=[Tb[b0]])
                if is_rope:
                    S.op("act", cp(kb[idx % 2], bank(b0)), reads=[Tb[b0]], writes=[Tkb[idx % 2]])

            def stB(idx):
                st, dest, Td, is_rope = items[idx]
                b0 = 2 * (idx % 2)
                if is_rope:
                    S.op("pe", mm(bank(b0 + 1), Pm, kb[idx % 2]), reads=[TPm, Tkb[idx % 2]], writes=[Tb[b0 + 1]])
                    k = rk[0] % 2
                    rk[0] += 1
                    S.op("dve", tt(rt[k], bank(b0), cosT[:, st * 512:(st + 1) * 512], ALU.mult), reads=[Tb[b0], Trope, Tkb[idx % 2]], writes=[Trt[k]])
                    S.op("dve", tt(ru[k], bank(b0 + 1), sinT[:, st * 512:(st + 1) * 512], ALU.mult), reads=[Tb[b0 + 1], Trope], writes=[Tru[k]])
                    S.op("dve", tt(dest, rt[k], ru[k], ALU.add), reads=[Trt[k], Tru[k]], writes=[Td])
                else:
                    S.op("act", cp(dest, bank(b0)), reads=[Tb[b0]], writes=[Td])

            stA(0)
            for idx in range(n_it):
                if idx + 1 < n_it:
                    stA(idx + 1)
                stB(idx)

        kitems = [(st, kT[:, st * 512:(st + 1) * 512], TkT[st], True) for st in range(8)]
        kitems.append((8, kT[:, 4608:5120], TkT[9], False))
        proj_rope(384, kitems)
        if h == 0:
            chk(3.1)
        psb = PS[:, 2 * 512:2 * 512 + 256].bitcast(BF16)
        for i in range(4):
            S.op("pe", tr(psb[:, i * 128:(i + 1) * 128], kp_tok[:, i, :], ident_b), reads=[Tkp, Tconst], writes=[Tb[2]])
        S.op("act", cp(kT[:, 4096:4608], psb), reads=[Tb[2]], writes=[TkT[8]])
        if h == 0:
            chk(3.2)
        for t4 in range(9):
            bk = 2 + (t4 % 2)
            isc = t4 == 8
            ncol = 256 if isc else 128
            for i in range(4):
                t = 4 * t4 + i
                for kc in range(8):
                    if isc:
                        S.op("pe", mm(bank(2 + i // 2, 256, (i % 2) * 256),
                                      hT[:, kc, t * 128:(t + 1) * 128], W[:, kc, 384:640], kc == 0, kc == 7),
                             reads=[Tw, ThT[t]], writes=[Tb[2 + i // 2]])
                    else:
                        S.op("pe", mm(bank(bk, 128, i * 128), hT[:, kc, t * 128:(t + 1) * 128], W[:, kc, 512:640], kc == 0, kc == 7),
                             reads=[Tw, ThT[t]], writes=[Tb[bk]])
            if not isc:
                S.op("act", cp(v_tok[:, 4 * t4:4 * t4 + 4, :], bank(bk).rearrange("p (a b) -> p a b", a=4)), reads=[Tb[bk]], writes=[Tv[t4]])
            else:
                for i in range(4):
                    t = 32 + i
                    src = bank(2 + i // 2, 256, (i % 2) * 256)
                    kk = i % 2
                    S.op("dve", cp(kvst[kk], src), reads=[Tb[2 + i // 2]], writes=[Tkvst[kk]])
                    S.op("act", cp(v_tok[:, 36 + i, :], kvst[kk][:, 128:256]), reads=[Tkvst[kk]], writes=[Tv[9]])
                    S.dma_store("sp", Tkvst[kk], [(d_ko[i * 128:(i + 1) * 128, h, :], kvst[kk][:, 0:128]),
                                                  (d_vo[i * 128:(i + 1) * 128, h, :], kvst[kk][:, 128:256])])
        if h == 0:
            chk(3.3)
        qitems = [(st, qT[:, oi * 512:(oi + 1) * 512], TqT[oi], st < 8) for oi, st in enumerate(OWN_STS)]
        proj_rope(0, qitems)
        for oi, st in enumerate(OWN_STS):
            gb = 4 + (oi % 2)
            pg = bank(gb)
            for kc in range(8):
                S.op("pe", mm(pg, W[:, kc, 640:768], hT[:, kc, st * 512:(st + 1) * 512], kc == 0, kc == 7),
                     reads=[Tw] + hT_st(st), writes=[Tb[gb]])
            S.op("act", act(sgT[:, oi * 512:(oi + 1) * 512], pg, AF.Silu), reads=[Tb[gb]], writes=[Tsg[oi]])
        if h == 0:
            chk(3.4)
        jobs = []
        for qb in range(4):
            keys = [(kt * 128, kt, TkT[kt // 4], Tv[kt // 4]) for kt in range(36)]
            jobs.append((qb * 512, 512, keys, TqT[qb], Tsg[qb]))
        for ci in range(2):
            keys = [(4608 + ci * 256 + kk * 128, 36 + 2 * ci + kk, TkT[9], Tv[9]) for kk in range(2)]
            jobs.append((2048 + ci * 256, 256, keys, TqT[4], Tsg[4]))
        flat = []
        for ji, (q0, N, keys, Tq, Tsgq) in enumerate(jobs):
            for ki in range(len(keys)):
                flat.append((ji, ki))

        def emit_qk_exp(g):
            ji, ki = flat[g]
            q0, N, keys, Tq, Tsgq = jobs[ji]
            kc0, vt, Tk_, Tv_ = keys[ki]
            sb_ = g % 2
            pS = PS[:, sb_ * 1024:(sb_ + 1) * 1024]
            pSv = pS.rearrange("p (c n) -> p c n", c=2)
            for cpn in range(2):
                ps_ = slice(cpn * 64, (cpn + 1) * 64)
                S.op("pe", mm(pSv[:, cpn, 0:N], kT[ps_, kc0:kc0 + 128], qT[ps_, q0:q0 + N]),
                     reads=[Tk_, Tq], writes=[Tb[2 * sb_ + cpn]])
            pb_ = g % 3
            PTv = PT[pb_].rearrange("p (c n) -> p c n", c=2)
            if N == 512:
                S.op("act", act(PT[pb_], pS, AF.Exp, scale=0.125), reads=[Tb[2 * sb_], Tb[2 * sb_ + 1]], writes=[TPT[pb_]])
            else:
                S.op("act", act(PTv[:, :, 0:N], pSv[:, :, 0:N], AF.Exp, scale=0.125), reads=[Tb[2 * sb_], Tb[2 * sb_ + 1]], writes=[TPT[pb_]])

        def emit_pv(g):
            ji, ki = flat[g]
            q0, N, keys, Tq, Tsgq = jobs[ji]
            kc0, vt, Tk_, Tv_ = keys[ki]
            sb_ = g % 3
            PTv = PT[sb_].rearrange("p (c n) -> p c n", c=2)
            pO = [bank(4, N), bank(5, N)]
            pL = [bank(6, N), bank(7, N)]
            first, lastk = ki == 0, ki == len(keys) - 1
            for cpn in range(2):
                S.op("pe", mm(pO[cpn], v_tok[:, vt, :], PTv[:, cpn, 0:N], first, lastk), reads=[Tv_, TPT[sb_]], writes=[TpO[cpn]])
                S.op("pe", mm(pL[cpn], ones_b, PTv[:, cpn, 0:N], first, lastk), reads=[Tconst, TPT[sb_]], writes=[TpL[cpn]])
            if not lastk:
                return
            qs = slice(q0, q0 + N)
            pLv = PS[:, 6 * 512:8 * 512].rearrange("p (c n) -> p c n", c=2)[:, :, 0:N]
            pOv = PS[:, 4 * 512:6 * 512].rearrange("p (c n) -> p c n", c=2)[:, :, 0:N]
            S.op("act", act(rr[:, :, 0:N], pLv, AF.Ln), reads=[TpL[0], TpL[1]], writes=[Tr0, Tr1])
            S.op("act", act(rr[:, :, 0:N], rr[:, :, 0:N], AF.Exp, scale=-1.0), reads=[Tr0, Tr1], writes=[Tr0, Tr1])
            S.op("dve", tt(tta[:, :, 0:N], pOv, rr[:, :, 0:N], ALU.mult), reads=[TpO[0], TpO[1], Tr0, Tr1], writes=[Tt0, Tt1])
            S.op("dve", stt(att[:, 0:N], t1a[:, 0:N], neglam[:, 0:1], t0a[:, 0:N], ALU.mult, ALU.add), reads=[Tt0, Tt1, Tsm], writes=[Tatt])
            S.op("dve", tt(sqa[:, 0:N], att[:, 0:N], att[:, 0:N], ALU.mult), reads=[Tatt], writes=[Tsqa])
            pN = bank(6, N)
            S.op("pe", mm(pN, ones_b, sqa[:, 0:N]), reads=[Tconst, Tsqa], writes=[TpL[0]])
            S.op("act", act(rsd[:, 0:N], pN, AF.Ln, scale=1.0 / 128, bias=EPS), reads=[TpL[0]], writes=[Trsd])
            S.op("act", act(rsd[:, 0:N], rsd[:, 0:N], AF.Exp, scale=-0.5), reads=[Trsd], writes=[Trsd])
            S.op("dve", stt(o1[:, 0:N], att[:, 0:N], sublnc[:, 0:1], rsd[:, 0:N], ALU.mult, ALU.mult), reads=[Tatt, Tsm, Trsd], writes=[To1])
            S.op("dve", tt(YTa[:, qs], o1[:, 0:N], sgT[:, qs], ALU.mult), reads=[To1, Tsgq], writes=[TYTa])

        G_ = len(flat)
        for g in range(G_ + 2):
            if g < G_:
                emit_qk_exp(g)
            if g >= 2:
                emit_pv(g - 2)
        S.dma_store("sp", TYTa, [(d_yt[h], YTa)], dram_t=Tyt[h])
        chk(3.7 + 0.01 * h)
    S.barrier()
    A.release(m2)

    chk(4)
    def h_alloc(shape, dt):
        return A.alloc(shape, dt)

    wo = h_alloc([128, 16, 1024], BF16)
    Two = T("wo")
    gbc = [h_alloc([128, 1024], F32) for _ in range(2)]
    Tgbc = T("gbc")
    fgb = h_alloc([128, 1024], F32)
    Tfg = T("fgb")
    ytl = [h_alloc([128, 16, 128], BF16) for _ in range(2)]
    Tytl = [T("ytl0"), T("ytl1")]
    xr = [h_alloc([128, 1024], F32) for _ in range(2)]
    Txr = [T("xr0"), T("xr1")]
    oa = h_alloc([128, 1024], F32)
    Toa = T("oa")
    ob = h_alloc([128, 1024], F32)
    Tob = T("ob")
    yo = [h_alloc([128, 1024], F32) for _ in range(2)]
    Tyo = [T("yo0"), T("yo1")]
    junk5 = h_alloc([128, 1024], BF16)
    Tj5 = T("junk5")
    rs5 = h_alloc([128, 20], F32)
    Trs5 = T("rs5")
    st5 = h_alloc([128, 3, 20], F32)
    Tst5 = [T("st5_%d" % i) for i in range(20)]

    S.dma_load("pool", Two, [(wo, d_wout.rearrange("(b p) n -> p b n", p=128))])
    S.dma_load("sp", Tgbc, [(gbc[c], d_mscr[c, 2048:3072].partition_broadcast(128)) for c in range(2)], reads=[Tmscr])
    S.dma_load("sp", Tfg, [(fgb, d_fg)])
    S.op("act", act(rs5, ssq_tok, AF.Ln, scale=1.0 / 1024, bias=EPS), reads=[Tssqt], writes=[Trs5])
    S.op("act", act(rs5, rs5, AF.Exp, scale=-0.5), reads=[Trs5], writes=[Trs5])
    ytv = d_yt.rearrange("b p t -> p b t")
    def ph5_A(ot):
        sl = ot % 2
        b0 = 4 * (ot % 2)
        S.dma_load("sp", Tytl[sl], [(ytl[sl], ytv[:, :, ot * 128:(ot + 1) * 128])], reads=Tyt)
        src = d_xs[ot * 128:(ot + 1) * 128, :] if ot < 16 else d_xc[(ot - 16) * 128:(ot - 15) * 128, :]
        S.dma_load("sp", Txr[sl], [(xr[sl], src)])
        for half in range(2):
            for b in range(8):
                S.op("pe", mm(bank(b0 + half), ytl[sl][:, b, :], wo[:, b, half * 512:(half + 1) * 512], b == 0, b == 7),
                     reads=[Tytl[sl], Two], writes=[Tb[b0 + half]])
            for b in range(8, 16):
                S.op("pe", mm(bank(b0 + 2 + half), ytl[sl][:, b, :], wo[:, b, half * 512:(half + 1) * 512], b == 8, b == 15),
                     reads=[Tytl[sl], Two], writes=[Tb[b0 + 2 + half]])

    def ph5_B(ot):
        sl = ot % 2
        b0 = 4 * (ot % 2)
        c = 0 if ot < 16 else 1
        S.op("act", cp(oa, PS[:, b0 * 512:b0 * 512 + 1024]), reads=[Tb[b0], Tb[b0 + 1]], writes=[Toa])
        S.op("dve", stt(ob, PS[:, (b0 + 2) * 512:(b0 + 2) * 512 + 1024], rs5[:, ot:ot + 1], oa, ALU.mult, ALU.add),
             reads=[Tb[b0 + 2], Tb[b0 + 3], Trs5, Toa], writes=[Tob])
        S.op("dve", tt(ob, ob, gbc[c], ALU.mult), reads=[Tob, Tgbc], writes=[Tob])
        S.op("dve", tt(ob, ob, xr[sl], ALU.add), reads=[Tob, Txr[sl]], writes=[Tob])
        S.op("act", act(junk5, ob, AF.Square, accum_out=st5[:, 0, ot:ot + 1]), reads=[Tob], writes=[Tj5, Tst5[ot]])
        S.op("act", act(st5[:, 1, ot:ot + 1], st5[:, 0, ot:ot + 1], AF.Ln, scale=1.0 / 1024, bias=EPS), reads=[Tst5[ot]], writes=[Tst5[ot]])
        S.op("act", act(st5[:, 2, ot:ot + 1], st5[:, 1, ot:ot + 1], AF.Exp, scale=-0.5), reads=[Tst5[ot]], writes=[Tst5[ot]])
        S.op("dve", stt(yo[sl], ob, st5[:, 2, ot:ot + 1], fgb, ALU.mult, ALU.mult), reads=[Tob, Tst5[ot], Tfg], writes=[Tyo[sl]])
        dst = d_ys[ot * 128:(ot + 1) * 128, :] if ot < 16 else d_yc[(ot - 16) * 128:(ot - 15) * 128, :]
        S.dma_store("sp", Tyo[sl], [(dst, yo[sl])])

    ph5_A(0)
    for ot in range(20):
        if ot + 1 < 20:
            ph5_A(ot + 1)
        ph5_B(ot)

    return


def _rope_tables():
    f32 = np.float32
    inv = (np.float32(10000.0) ** (-np.arange(16, dtype=f32) / np.float32(16))).astype(f32)
    t = np.arange(4096)
    row = (t // 64).astype(f32)
    col = (t % 64).astype(f32)
    f = np.arange(128)
    hf = (f % 64) // 32
    two = (f % 32) // 16
    i = f % 16
    pos = np.where(hf[:, None] == 0, row[None, :], col[None, :]).astype(f32)
    ang = (pos * inv[i][:, None]).astype(f32)
    cosT = np.cos(ang).astype(f32)
    sinT = (np.sin(ang) * np.where(two == 0, -1.0, 1.0)[:, None]).astype(f32)
    return cosT, sinT


def _prep_inputs(x_prompt, x_sample, cache_k, cache_v, state_ssd, c, c_ctx, w_mod, b_mod, norm_g, w_in,
                 lambda_q1, lambda_k1, lambda_q2, lambda_k2, subln_g, conv_w, conv_b, dt_bias, A_log,
                 D_skip, ssd_norm_g, w_out, final_g):
    f32 = np.float32
    A_ = lambda a: np.ascontiguousarray(np.asarray(a, dtype=f32))
    x_prompt, x_sample = A_(x_prompt), A_(x_sample)
    cache_k, cache_v, state_ssd = A_(cache_k), A_(cache_v), A_(state_ssd)
    c, c_ctx = A_(c), A_(c_ctx)
    w_in0 = A_(w_in)[0]
    wq, wk, wv, wg = w_in0[:, 0:1024], w_in0[:, 1024:2048], w_in0[:, 2048:3072], w_in0[:, 3072:4096]
    wz, wx = w_in0[:, 4096:5120], w_in0[:, 5120:6144]
    wB, wC, wdt = w_in0[:, 6144:6400], w_in0[:, 6400:6656], w_in0[:, 6656:6688]
    perm = np.arange(128) ^ 16
    cols = [wB, wC, wdt, wdt]
    for j in range(8):
        cols += [wx[:, j * 128:(j + 1) * 128], wz[:, j * 128:(j + 1) * 128]]
    for h in range(8):
        qh, kh = wq[:, h * 128:(h + 1) * 128], wk[:, h * 128:(h + 1) * 128]
        cols += [qh, qh[:, perm], kh[:, perm], kh, wv[:, h * 128:(h + 1) * 128], wg[:, h * 128:(h + 1) * 128]]
    wall0 = np.ascontiguousarray(np.concatenate(cols, axis=1))
    assert wall0.shape == (1024, NCOL)
    wall1 = wall0.copy()
    wall1[:, 512:528] = wdt[:, 16:32]
    wall1[:, 528:544] = wdt[:, 0:16]
    cosT, sinT = _rope_tables()
    cosF, sinF = np.ascontiguousarray(cosT[:, ::-1]), np.ascontiguousarray(sinT[:, ::-1])
    wmod0 = A_(w_mod)[0]
    bmod2 = np.ascontiguousarray(np.broadcast_to(A_(b_mod)[0][None, :], (2, 3072)))
    wout0 = A_(w_out)[0]
    fgbc = np.ascontiguousarray(np.broadcast_to(A_(final_g)[None, :], (128, 1024)))
    ng = A_(norm_g)[0]
    cw = A_(conv_w)[0]
    cb = A_(conv_b)[0]
    dtb = A_(dt_bias)[0]
    al = A_(A_log)[0]
    Dk = A_(D_skip)[0]
    sg = A_(ssd_norm_g)[0]
    sub = A_(subln_g)[0]
    lam = np.stack([A_(lambda_q1)[0], A_(lambda_k1)[0], A_(lambda_q2)[0], A_(lambda_k2)[0]], 0)

    def colmaj(v, nb):
        return v.reshape(nb, 128).T

    in_maps = []
    for core in range(NCORES):
        b, half = core // 2, core % 2
        cst = np.zeros((128, NCST), f32)
        cst[:, C_NG:C_NG + 8] = colmaj(ng, 8)
        cwn = np.stack([colmaj(cw[j], 12) for j in range(3)], axis=2)
        cws = cwn[:, :, ::-1] if half else cwn
        cst[:, C_CWS:C_CWS + 36] = cws.reshape(128, 36)
        cst[:, C_CWC:C_CWC + 36] = cwn.reshape(128, 36)
        cst[:, C_CB:C_CB + 12] = colmaj(cb, 12)
        dsel = [1, 0] if half else [0, 1]
        cst[:, C_DTBS:C_DTBS + 32] = dtb[dsel].reshape(32)[None, :]
        cst[:, C_DTBC:C_DTBC + 32] = dtb.reshape(32)[None, :]
        cst[:, C_ALS:C_ALS + 32] = al[dsel].reshape(32)[None, :]
        cst[:, C_ALC:C_ALC + 32] = al.reshape(32)[None, :]
        cst[:, C_D:C_D + 8] = np.repeat(Dk.reshape(8, 2).T, 64, axis=0)
        cst[:, C_SG:C_SG + 8] = colmaj(sg, 8)
        cst[:, C_SUB] = sub
        cst[:, C_LAM:C_LAM + 256] = lam.reshape(256)[None, :]
        xs = x_sample[b]
        if half:
            xs = np.ascontiguousarray(xs[::-1])
        cond2 = np.stack([colmaj(c[b], 8), colmaj(c_ctx, 8)], axis=2)
        h0 = state_ssd[b, 0]
        if half:
            h0 = h0[::-1]
        in_maps.append({
            "xs": np.ascontiguousarray(xs),
            "xc": np.ascontiguousarray(x_prompt[2 * core:2 * core + 2].reshape(512, 1024)),
            "cond2": np.ascontiguousarray(cond2),
            "wmod": wmod0,
            "bmod2": bmod2,
            "wall": wall1 if half else wall0,
            "wout": wout0,
            "cosT": cosF if half else cosT,
            "sinT": sinF if half else sinT,
            "ck": np.ascontiguousarray(cache_k[b, 0]),
            "cv": np.ascontiguousarray(cache_v[b, 0]),
            "h0": np.ascontiguousarray(h0),
            "cst": cst,
            "fgbc": fgbc,
        })
    return in_maps


_NC_CACHE = {}
_STATS = {}


def kernel(**inputs):
    in_maps = _prep_inputs(**inputs)
    if "nc" not in _NC_CACHE:
        _NC_CACHE["nc"] = build_program()
    nc = _NC_CACHE["nc"]
    res = run_bass_kernel_spmd(nc, in_maps, core_ids=list(range(NCORES)))
    f32 = np.float32
    y_prompt = np.zeros((16, 256, 1024), f32)
    y_sample = np.zeros((4, 4096, 1024), f32)
    new_k = np.zeros((16, 1, 256, 8, 128), f32)
    new_v = np.zeros((16, 1, 256, 8, 128), f32)
    new_h = np.zeros((16, 1, 2, 16, 64, 64), f32)
    for core in range(NCORES):
        r = res.results[core]
        b, half = core // 2, core % 2
        ys = np.asarray(r["ys"], dtype=f32)
        if half:
            y_sample[b, 2048:4096] = ys[::-1]
        else:
            y_sample[b, 0:2048] = ys
        y_prompt[2 * core:2 * core + 2] = np.asarray(r["yc"], dtype=f32).reshape(2, 256, 1024)
        new_k[2 * core:2 * core + 2, 0] = np.asarray(r["ko"], dtype=f32).reshape(2, 256, 8, 128)
        new_v[2 * core:2 * core + 2, 0] = np.asarray(r["vo"], dtype=f32).reshape(2, 256, 8, 128)
        new_h[2 * core:2 * core + 2, 0] = np.asarray(r["ho"], dtype=f32)
    return (y_prompt, y_sample, new_k, new_v, new_h)
```
